# Optimizing a Trainium2 kernel written in Bass

```python
import math
import numpy as np
import jax
import jax.numpy as jnp
from jax import lax


D_MODEL = 2048
BATCH = 1
SEQ = 8192
DEPTH = 2

GRID_W = 64
CTX_LEN = 256
EPS = 1e-6
ROPE_BASE = 10000.0

SSD_EXPAND = 2
D_INNER = SSD_EXPAND * D_MODEL
SSD_HEADDIM = 64
SSD_HEADS = D_INNER // SSD_HEADDIM
SSD_GROUPS = 8
SSD_HPG = SSD_HEADS // SSD_GROUPS
SSD_STATE = 128
SSD_CONV = 5
SSD_CHUNK = 128
CONV_DIM = D_INNER + 2 * SSD_GROUPS * SSD_STATE

NA_HEAD_DIM = 128
NA_HEADS = D_MODEL // NA_HEAD_DIM
NA_WIDTH = NA_HEADS * NA_HEAD_DIM
NA_WIN_H = 8
NA_WIN_W = 16
NA_QCB = NA_WIN_W
NA_KCB = 2 * NA_WIN_W
NA_NCB = GRID_W // NA_QCB

D_FF = 4 * D_MODEL

IN_SIZES = (D_INNER, CONV_DIM, 2 * SSD_HEADS, NA_WIDTH, NA_WIDTH, NA_WIDTH, D_MODEL, D_MODEL)
IN_DIM = D_INNER + CONV_DIM + 2 * SSD_HEADS + 3 * NA_WIDTH + 2 * D_MODEL

kernel_name = 'hybrid_ssd_natten_dit_block'


def rmsnorm(x, g):
    x32 = x.astype(jnp.float32)
    y = x32 * lax.rsqrt(jnp.mean(x32 * x32, axis=-1, keepdims=True) + EPS)
    return y.astype(x.dtype) * g


def modulate(x, shift, scale):
    return x * (1 + scale) + shift


def split_cols(p):
    return jnp.split(p, np.cumsum(IN_SIZES)[:-1].tolist(), axis=-1)


def axial_rope(x, row, col):
    n_ax = x.shape[-1] // 2
    inv = ROPE_BASE ** (-jnp.arange(0, n_ax, 2, dtype=jnp.float32) / n_ax)

    def rot(xa, pos):
        ang = pos.astype(jnp.float32)[:, None] * inv
        cos = jnp.cos(ang)[:, None, :].astype(x.dtype)
        sin = jnp.sin(ang)[:, None, :].astype(x.dtype)
        x1, x2 = jnp.split(xa, 2, axis=-1)
        return jnp.concatenate([x1 * cos - x2 * sin, x2 * cos + x1 * sin], axis=-1)

    xr, xc = jnp.split(x, 2, axis=-1)
    return jnp.concatenate([rot(xr, row), rot(xc, col)], axis=-1)


def dwconv_centred(x, w, b):
    k = w.shape[0]
    y = lax.conv_general_dilated(x, w[:, None, :].astype(x.dtype), window_strides=(1,),
                                 padding=((k // 2, k // 2),),
                                 dimension_numbers=('NWC', 'WIO', 'NWC'),
                                 feature_group_count=x.shape[-1])
    return y + b


def gated_rmsnorm(y, z, w):
    g = (y * jax.nn.silu(z)).astype(jnp.float32)
    gs = g.reshape(g.shape[:-1] + (SSD_GROUPS, -1))
    gs = gs * lax.rsqrt(jnp.mean(gs * gs, axis=-1, keepdims=True) + EPS)
    return gs.reshape(g.shape).astype(y.dtype) * w


def segsum(a):
    L = a.shape[-1]
    xa = jnp.broadcast_to(a[..., :, None], a.shape + (L,))
    xa = jnp.where(jnp.tril(jnp.ones((L, L), dtype=bool), -1), xa, 0.0)
    cs = jnp.cumsum(xa, axis=-2)
    return jnp.where(jnp.tril(jnp.ones((L, L), dtype=bool), 0), cs, -jnp.inf)


def ssd_scan(xh, dt, a, bm, cm, h0):
    b_, T = xh.shape[0], xh.shape[1]
    nc, L = T // SSD_CHUNK, SSD_CHUNK
    dtype = xh.dtype
    da = (dt * a).reshape(b_, nc, L, SSD_GROUPS, SSD_HPG)
    da = jnp.transpose(da, (0, 1, 3, 4, 2))
    da_cs = jnp.cumsum(da, axis=-1)
    xdt = (xh * dt[..., None].astype(dtype)).reshape(b_, nc, L, SSD_GROUPS, SSD_HPG, SSD_HEADDIM)
    bc = bm.reshape(b_, nc, L, SSD_GROUPS, SSD_STATE)
    cc = cm.reshape(b_, nc, L, SSD_GROUPS, SSD_STATE)
    decay_in = jnp.exp(segsum(da)).astype(dtype)
    cb = jnp.einsum('bclgn,bcsgn->bcgls', cc, bc)
    y_diag = jnp.einsum('bcgls,bcgels,bcsgep->bclgep', cb, decay_in, xdt)
    decay_to_end = jnp.exp(da_cs[..., -1:] - da_cs).astype(dtype)
    chunk_states = jnp.einsum('bclgn,bcgel,bclgep->bcgepn', bc, decay_to_end, xdt)
    chunk_decay = jnp.exp(da_cs[..., -1]).astype(dtype)

    def step(h, inp):
        dec, st = inp
        return dec[..., None, None] * h + st, h

    h_final, h_in = lax.scan(step, h0.astype(dtype),
                             (jnp.moveaxis(chunk_decay, 1, 0), jnp.moveaxis(chunk_states, 1, 0)))
    h_in = jnp.moveaxis(h_in, 0, 1)
    y_off = jnp.einsum('bclgn,bcgepn,bcgel->bclgep', cc, h_in, jnp.exp(da_cs).astype(dtype))
    y = (y_diag + y_off).reshape(b_, T, SSD_HEADS, SSD_HEADDIM)
    return y, h_final


def ssd_branch(z_l, xbc_l, dt_l, z_c, xbc_c, dt_c, conv_w, conv_b, a_log, dt_bias, d_skip,
               norm_w, w_o, row, col, with_ctx_out):
    a = -jnp.exp(a_log.astype(jnp.float32))

    def prep(xbc, dt_raw, use_rope):
        b_, T = xbc.shape[0], xbc.shape[1]
        xbc = jax.nn.silu(dwconv_centred(xbc, conv_w, conv_b))
        xs, bm, cm = jnp.split(xbc, [D_INNER, D_INNER + SSD_GROUPS * SSD_STATE], axis=-1)
        xs = xs.reshape(b_, T, SSD_HEADS, SSD_HEADDIM)
        bm = bm.reshape(b_, T, SSD_GROUPS, SSD_STATE)
        cm = cm.reshape(b_, T, SSD_GROUPS, SSD_STATE)
        if use_rope:
            bm, cm = axial_rope(bm, row, col), axial_rope(cm, row, col)
        dt = jax.nn.softplus(dt_raw.astype(jnp.float32).reshape(b_, T, 2, SSD_HEADS) + dt_bias)
        return xs, bm, cm, dt

    xc_, bc_, cc_, dtc = prep(xbc_c, dt_c, False)
    xl_, bl_, cl_, dtl = prep(xbc_l, dt_l, True)
    b_ = xl_.shape[0]
    h0 = jnp.zeros((b_, SSD_GROUPS, SSD_HPG, SSD_HEADDIM, SSD_STATE), xl_.dtype)
    flip = lambda t: jnp.flip(t, axis=1)
    y_cf, h_cf = ssd_scan(xc_, dtc[:, :, 0], a[0], bc_, cc_, h0)
    y_cb, h_cb = ssd_scan(flip(xc_), flip(dtc[:, :, 1]), a[1], flip(bc_), flip(cc_), h0)
    y_lf, _ = ssd_scan(xl_, dtl[:, :, 0], a[0], bl_, cl_, h_cf)
    y_lb, _ = ssd_scan(flip(xl_), flip(dtl[:, :, 1]), a[1], flip(bl_), flip(cl_), h_cb)
    d_sum = (d_skip[0] + d_skip[1])[:, None].astype(xl_.dtype)

    def finish(y_f, y_b, xs, z):
        y = y_f + y_b + d_sum * xs
        y = gated_rmsnorm(y.reshape(xs.shape[0], xs.shape[1], D_INNER), z, norm_w)
        return y @ w_o

    y_lat = finish(y_lf, flip(y_lb), xl_, z_l)
    y_ctx = finish(y_cf, flip(y_cb), xc_, z_c) if with_ctx_out else None
    return y_lat, y_ctx


def neighbourhood_attention(q, k, v, k_ctx, v_ctx, rpb):
    b_, T, H, d = q.shape
    rows = T // GRID_W
    wh = min(NA_WIN_H, rows)
    qg = q.reshape(b_, rows, GRID_W, H, d)
    kg = k.reshape(b_, rows, GRID_W, H, d)
    vg = v.reshape(b_, rows, GRID_W, H, d)
    qcol = np.arange(GRID_W).reshape(NA_NCB, NA_QCB)
    kstart = np.clip(np.arange(NA_NCB) * NA_QCB - NA_WIN_W // 2, 0, GRID_W - NA_KCB)
    kcol = kstart[:, None] + np.arange(NA_KCB)[None, :]
    cstart = np.clip(qcol - NA_WIN_W // 2, 0, GRID_W - NA_WIN_W)
    col_ok = (kcol[:, None, :] >= cstart[..., None]) & (kcol[:, None, :] < cstart[..., None] + NA_WIN_W)
    dcol_idx = np.clip(kcol[:, None, :] - qcol[:, :, None], -(NA_WIN_W - 1), NA_WIN_W - 1) + NA_WIN_W - 1
    mask = np.broadcast_to(col_ok[:, :, None, :], (NA_NCB, NA_QCB, wh, NA_KCB)).reshape(NA_NCB, NA_QCB, wh * NA_KCB)
    nk = wh * NA_KCB
    scale = d ** -0.5

    def row_block(r):
        rs = jnp.clip(r - NA_WIN_H // 2, 0, rows - wh)
        qr = lax.dynamic_index_in_dim(qg, r, axis=1, keepdims=False).reshape(b_, NA_NCB, NA_QCB, H, d)
        kb = lax.dynamic_slice_in_dim(kg, rs, wh, axis=1)[:, :, kcol]
        vb = lax.dynamic_slice_in_dim(vg, rs, wh, axis=1)[:, :, kcol]
        kb = jnp.transpose(kb, (0, 2, 1, 3, 4, 5)).reshape(b_, NA_NCB, nk, H, d)
        vb = jnp.transpose(vb, (0, 2, 1, 3, 4, 5)).reshape(b_, NA_NCB, nk, H, d)
        drow_idx = rs + jnp.arange(wh) - r + NA_WIN_H - 1
        bias = rpb[:, drow_idx][:, :, dcol_idx]
        bias = jnp.transpose(bias, (0, 2, 3, 1, 4)).reshape(H, NA_NCB, NA_QCB, nk)
        s_loc = jnp.einsum('bjqhd,bjkhd->bhjqk', qr, kb).astype(jnp.float32) * scale + bias.astype(jnp.float32)
        s_loc = jnp.where(mask, s_loc, -jnp.inf)
        s_ctx = jnp.einsum('bjqhd,bkhd->bhjqk', qr, k_ctx).astype(jnp.float32) * scale
        p = jax.nn.softmax(jnp.concatenate([s_loc, s_ctx], axis=-1), axis=-1).astype(v.dtype)
        o = (jnp.einsum('bhjqk,bjkhd->bjqhd', p[..., :nk], vb)
             + jnp.einsum('bhjqk,bkhd->bjqhd', p[..., nk:], v_ctx))
        return o.reshape(b_, GRID_W, H, d)

    out = lax.map(row_block, jnp.arange(rows))
    return jnp.moveaxis(out, 0, 1).reshape(b_, T, H, d)


def context_attention(q, k, v):
    s = jnp.einsum('bqhd,bkhd->bhqk', q, k).astype(jnp.float32) * (q.shape[-1] ** -0.5)
    p = jax.nn.softmax(s, axis=-1).astype(v.dtype)
    return jnp.einsum('bhqk,bkhd->bqhd', p, v)


def mixer_block(u_l, u_c, w_in, conv_w, conv_b, a_log, dt_bias, d_skip, ssd_norm, w_ssd_o,
                rpb, w_na_o, w_out, row, col, with_ctx_out):
    z_l, xbc_l, dt_l, q_l, k_l, v_l, ga_l, gb_l = split_cols(u_l @ w_in)
    z_c, xbc_c, dt_c, q_c, k_c, v_c, ga_c, gb_c = split_cols(u_c @ w_in)
    y_ssd_l, y_ssd_c = ssd_branch(z_l, xbc_l, dt_l, z_c, xbc_c, dt_c, conv_w, conv_b, a_log, dt_bias,
                                  d_skip, ssd_norm, w_ssd_o, row, col, with_ctx_out)
    heads = lambda t: t.reshape(t.shape[0], t.shape[1], NA_HEADS, NA_HEAD_DIM)
    kc, vc = heads(k_c), heads(v_c)
    o_l = neighbourhood_attention(heads(q_l), heads(k_l), heads(v_l), kc, vc, rpb)
    y_na_l = o_l.reshape(u_l.shape[0], u_l.shape[1], NA_WIDTH) @ w_na_o
    out_l = (jax.nn.sigmoid(ga_l) * y_ssd_l + jax.nn.sigmoid(gb_l) * y_na_l) @ w_out
    if not with_ctx_out:
        return out_l, None
    o_c = context_attention(heads(q_c), kc, vc)
    y_na_c = o_c.reshape(u_c.shape[0], u_c.shape[1], NA_WIDTH) @ w_na_o
    out_c = (jax.nn.sigmoid(ga_c) * y_ssd_c + jax.nn.sigmoid(gb_c) * y_na_c) @ w_out
    return out_l, out_c


def sq_relu_mlp(h, w1, w2):
    return jnp.square(jax.nn.relu(h @ w1)) @ w2


def setup_inputs(seed: int = 0) -> dict:
    key = jax.random.key(seed)
    ks = jax.random.split(key, 24)
    f32 = jnp.float32
    L = DEPTH

    def nrm(k, shape, s):
        return jax.random.normal(k, shape, f32) * s

    def gain(k, shape):
        return 1.0 + 0.05 * jax.random.normal(k, shape, f32)

    dt0 = jnp.exp(jax.random.uniform(ks[14], (L, 2, SSD_HEADS), f32, math.log(1e-3), math.log(1e-1)))
    dt_bias = dt0 + jnp.log(-jnp.expm1(-dt0))
    return {
        'x': nrm(ks[0], (BATCH, SEQ, D_MODEL), 1.0),
        'c': nrm(ks[1], (BATCH, D_MODEL), 1.0),
        'ctx': nrm(ks[2], (BATCH, CTX_LEN, D_MODEL), 1.0),
        'c_ctx': nrm(ks[3], (D_MODEL,), 1.0),
        'w_ada': nrm(ks[4], (L, D_MODEL, 6 * D_MODEL), D_MODEL ** -0.5),
        'b_ada': nrm(ks[5], (L, 6 * D_MODEL), 0.02),
        'g_pre_mix': gain(ks[6], (L, D_MODEL)),
        'g_post_mix': gain(ks[7], (L, D_MODEL)),
        'g_pre_mlp': gain(ks[8], (L, D_MODEL)),
        'g_post_mlp': gain(ks[9], (L, D_MODEL)),
        'w_in': nrm(ks[10], (L, D_MODEL, IN_DIM), D_MODEL ** -0.5),
        'conv_w': nrm(ks[11], (L, SSD_CONV, CONV_DIM), SSD_CONV ** -0.5),
        'conv_b': nrm(ks[12], (L, CONV_DIM), 0.02),
        'a_log': jnp.log(jax.random.uniform(ks[13], (L, 2, SSD_HEADS), f32, 1.0, 16.0)),
        'dt_bias': dt_bias,
        'd_skip': gain(ks[15], (L, 2, SSD_HEADS)),
        'ssd_norm': gain(ks[16], (L, D_INNER)),
        'w_ssd_o': nrm(ks[17], (L, D_INNER, D_MODEL), D_INNER ** -0.5),
        'rpb': nrm(ks[18], (L, NA_HEADS, 2 * NA_WIN_H - 1, 2 * NA_WIN_W - 1), 0.1),
        'w_na_o': nrm(ks[19], (L, NA_WIDTH, D_MODEL), NA_WIDTH ** -0.5),
        'w_out': nrm(ks[20], (L, D_MODEL, D_MODEL), D_MODEL ** -0.5),
        'w_mlp1': nrm(ks[21], (L, D_MODEL, D_FF), D_MODEL ** -0.5),
        'w_mlp2': nrm(ks[22], (L, D_FF, D_MODEL), D_FF ** -0.5),
    }


def reference(x, c, ctx, c_ctx, w_ada, b_ada, g_pre_mix, g_post_mix, g_pre_mlp, g_post_mlp, w_in,
              conv_w, conv_b, a_log, dt_bias, d_skip, ssd_norm, w_ssd_o, rpb, w_na_o, w_out,
              w_mlp1, w_mlp2):
    T = x.shape[1]
    t = jnp.arange(T, dtype=jnp.int32)
    row, col = t // GRID_W, t % GRID_W
    for l in range(DEPTH):
        last = l == DEPTH - 1
        mod = jax.nn.silu(c) @ w_ada[l] + b_ada[l]
        mod_c = jax.nn.silu(c_ctx) @ w_ada[l] + b_ada[l]
        sh1, sc1, gt1, sh2, sc2, gt2 = jnp.split(mod[:, None, :], 6, axis=-1)
        csh1, csc1, cgt1, csh2, csc2, cgt2 = jnp.split(mod_c, 6, axis=-1)
        u_l = modulate(rmsnorm(x, g_pre_mix[l]), sh1, sc1)
        u_c = modulate(rmsnorm(ctx, g_pre_mix[l]), csh1, csc1)
        y_l, y_c = mixer_block(u_l, u_c, w_in[l], conv_w[l], conv_b[l], a_log[l], dt_bias[l], d_skip[l],
                               ssd_norm[l], w_ssd_o[l], rpb[l], w_na_o[l], w_out[l], row, col, not last)
        x = x + gt1 * rmsnorm(y_l, g_post_mix[l])
        h = modulate(rmsnorm(x, g_pre_mlp[l]), sh2, sc2)
        x = x + gt2 * rmsnorm(sq_relu_mlp(h, w_mlp1[l], w_mlp2[l]), g_post_mlp[l])
        if not last:
            ctx = ctx + cgt1 * rmsnorm(y_c, g_post_mix[l])
            hc = modulate(rmsnorm(ctx, g_pre_mlp[l]), csh2, csc2)
            ctx = ctx + cgt2 * rmsnorm(sq_relu_mlp(hc, w_mlp1[l], w_mlp2[l]), g_post_mlp[l])
    return x
```

```python
import numpy as np
import concourse.bass as bass
import concourse.mybir as mybir

F32 = mybir.dt.float32
BF16 = mybir.dt.bfloat16
ALU = mybir.AluOpType
AF = mybir.ActivationFunctionType
AX = mybir.AxisListType

ENGS = ("tensor", "vector", "scalar", "gpsimd", "sync")


class Buf:
    __slots__ = ("name", "last_w", "readers")

    def __init__(self, name):
        self.name = name
        self.last_w = None
        self.readers = []


class Op:
    __slots__ = ("eng", "fn", "deps", "is_dma", "idx", "sem", "cum", "nodep_same")

    def __init__(self, eng, fn, is_dma):
        self.eng = eng
        self.fn = fn
        self.deps = []
        self.is_dma = is_dma
        self.idx = None
        self.sem = None
        self.cum = None


class Prog:
    def __init__(self, nc, n_dma_sems=6, same_engine_sync=True):
        self.nc = nc
        self.ops = []
        self.n_dma_sems = n_dma_sems
        self.same_engine_sync = same_engine_sync

    def add(self, eng, fn, reads=(), writes=(), dma=False):
        op = Op(eng, fn, dma)
        deps = []
        for b in reads:
            if b.last_w is not None:
                deps.append(b.last_w)
        for b in writes:
            if b.last_w is not None:
                deps.append(b.last_w)
            deps.extend(b.readers)
        seen = set()
        for d in deps:
            if id(d) not in seen and d is not op:
                seen.add(id(d))
                op.deps.append(d)
        for b in reads:
            b.readers.append(op)
        for b in writes:
            b.last_w = op
            b.readers = []
        self.ops.append(op)
        return op

    def emit(self, extra_final_wait=True):
        nc = self.nc
        per_eng = {e: [] for e in ENGS}
        for op in self.ops:
            per_eng[op.eng].append(op)
        import contextlib
        with contextlib.ExitStack() as st:
            eng_sem = {e: st.enter_context(nc.semaphore("s_" + e)) for e in ENGS}
            dma_sems = {e: [st.enter_context(nc.semaphore("d_%s%d" % (e, i)))
                            for i in range(self.n_dma_sems)] for e in ("sync", "scalar", "gpsimd")}
            cnt = {e: 0 for e in ENGS}
            dcnt = {e: 0 for e in ENGS}
            dsem_cum = {}
            dsem_prev = {}
            for e in ENGS:
                for op in per_eng[e]:
                    if op.is_dma:
                        k = dcnt[e] % self.n_dma_sems
                        dcnt[e] += 1
                        s = dma_sems[e][k]
                        key = (e, k)
                        prev = dsem_cum.get(key, 0)
                        op.sem = s
                        op.cum = prev + 16
                        dsem_prev[id(op)] = prev
                        dsem_cum[key] = prev + 16
                    else:
                        cnt[e] += 1
                        op.sem = eng_sem[e]
                        op.cum = cnt[e]
            block = st.enter_context(nc.Block())
            same_sync = self.same_engine_sync

            def make(e):
                def body(eng):
                    seen = {}
                    for op in per_eng[e]:
                        need = {}
                        for d in op.deps:
                            if d.eng == e and not d.is_dma:
                                if not same_sync or e == "tensor":
                                    continue
                            key = id(d.sem)
                            if need.get(key, (None, 0))[1] < d.cum:
                                need[key] = (d.sem, d.cum)
                        if op.is_dma:
                            prev = dsem_prev[id(op)]
                            if prev > 0:
                                key = id(op.sem)
                                if need.get(key, (None, 0))[1] < prev:
                                    need[key] = (op.sem, prev)
                        for key, (s, v) in need.items():
                            if seen.get(key, 0) >= v:
                                continue
                            eng.wait_ge(s, v)
                            seen[key] = v
                        ins = op.fn(eng)
                        if op.is_dma:
                            ins.then_inc(op.sem, 16)
                        else:
                            ins.then_inc(op.sem, 1)
                    if extra_final_wait:
                        for k in range(self.n_dma_sems):
                            key = (e, k)
                            if key in dsem_cum:
                                s = dma_sems[e][k]
                                if seen.get(id(s), 0) < dsem_cum[key]:
                                    eng.wait_ge(s, dsem_cum[key])
                return body

            for e in ENGS:
                if per_eng[e]:
                    getattr(block, e)(make(e))

import contextlib
import numpy as np
import ml_dtypes
import concourse.bass as bass
import concourse.mybir as mybir

EPS = 1e-6


class KB:
    def __init__(self, nc, P):
        self.nc = nc
        self.P = P

    def dma(self, eng, out, in_, reads=(), writes=()):
        return self.P.add(eng, lambda e: e.dma_start(out=out, in_=in_), reads, writes, dma=True)

    def mm(self, out, lhsT, rhs, start, stop, reads, writes):
        return self.P.add("tensor", lambda e: e.matmul(out, lhsT=lhsT, rhs=rhs, start=start, stop=stop), reads, writes)

    def tr(self, out, in_, ident, reads, writes):
        return self.P.add("tensor", lambda e: e.transpose(out, in_, ident), reads, writes)

    def act(self, out, in_, func, reads, writes, bias=None, scale=None, eng="scalar"):
        kw = {}
        if bias is not None:
            kw["bias"] = bias
        if scale is not None:
            kw["scale"] = scale
        return self.P.add(eng, lambda e: e.activation(out=out, in_=in_, func=func, **kw), reads, writes)

    def tt(self, eng, out, in0, in1, op, reads, writes):
        return self.P.add(eng, lambda e: e.tensor_tensor(out=out, in0=in0, in1=in1, op=op), reads, writes)

    def ts(self, eng, out, in0, s1, s2, op0, op1, reads, writes):
        if op1 is None:
            return self.P.add(eng, lambda e: e.tensor_scalar(out=out, in0=in0, scalar1=s1, scalar2=None, op0=op0), reads, writes)
        return self.P.add(eng, lambda e: e.tensor_scalar(out=out, in0=in0, scalar1=s1, scalar2=s2, op0=op0, op1=op1), reads, writes)

    def stt(self, eng, out, in0, scalar, in1, op0, op1, reads, writes):
        return self.P.add(eng, lambda e: e.scalar_tensor_tensor(out=out, in0=in0, scalar=scalar, in1=in1, op0=op0, op1=op1), reads, writes)

    def copy(self, eng, out, in_, reads, writes):
        if eng == "scalar":
            return self.P.add(eng, lambda e: e.activation(out=out, in_=in_, func=AF.Copy), reads, writes)
        return self.P.add(eng, lambda e: e.tensor_copy(out=out, in_=in_), reads, writes)

    def memset(self, eng, ap, val, writes):
        return self.P.add(eng, lambda e: e.memset(ap, val), (), writes)

    def rsum(self, eng, out, in_, reads, writes):
        return self.P.add(eng, lambda e: e.reduce_sum(out=out, in_=in_, axis=AX.X), reads, writes)


def ssd_consts():
    k = np.arange(128)
    c = {}
    c["trif"] = (k[:, None] <= k[None, :]).astype(np.float32)
    c["trib"] = (k[:, None] >= k[None, :]).astype(np.float32)
    c["maskf"] = (k[None, :] >= k[:, None]).astype(np.float32)
    c["maskb"] = (k[None, :] <= k[:, None]).astype(np.float32)
    c["identf"] = np.eye(128, dtype=np.float32)
    c["identb"] = np.eye(128, dtype=np.float32).astype(ml_dtypes.bfloat16)
    c["ones"] = np.ones((128, 128), np.float32)
    sel = np.zeros((16, 16, 128), np.float32)
    for j in range(16):
        sel[j, j, :] = 1.0
    c["sel"] = sel.reshape(16, 16 * 128)
    R = np.zeros((128, 128), np.float32)
    for base in (0, 64):
        for i in range(32):
            R[base + i, base + 32 + i] = -1.0
            R[base + 32 + i, base + i] = 1.0
    c["rt"] = np.ascontiguousarray(R.T)
    return c


def rope_tables(ntok, grid_w=64, base=10000.0):
    t = np.arange(ntok)
    row, col = t // grid_w, t % grid_w
    n_ax = 64
    inv = (base ** (-np.arange(0, n_ax, 2, dtype=np.float32) / n_ax)).astype(np.float32)
    cos = np.zeros((128, ntok), np.float32)
    sin = np.zeros((128, ntok), np.float32)
    for off, pos in ((0, row), (64, col)):
        ang = pos.astype(np.float32)[None, :] * inv[:, None]
        cs, sn = np.cos(ang).astype(np.float32), np.sin(ang).astype(np.float32)
        cos[off:off + 32] = cs
        cos[off + 32:off + 64] = cs
        sin[off:off + 32] = sn
        sin[off + 32:off + 64] = sn
    return cos, sin


def build_ssd(NCTX=2, NLAT=64):
    NCH = NCTX + NLAT
    nc = bass.Bass("TRN2", target_bir_lowering=False)
    din = lambda name, shape, dt: nc.dram_tensor(name, shape, dt, kind="ExternalInput").ap()
    uT = din("uT", [NCH, 128, 16, 132], BF16)
    w = din("w", [2048, 1296], F32)
    convw = din("convw", [128, 6, 5], F32)
    convb = din("convb", [128, 6], F32)
    hv = din("hv", [3, 16], F32)
    normw = din("normw", [512], F32)
    cosd = din("cos", [128, NLAT * 128], F32)
    sind = din("sin", [128, NLAT * 128], F32)
    cn = {n: din(n, list(s), dt) for n, s, dt in [
        ("trif", (128, 128), F32), ("trib", (128, 128), F32), ("maskf", (128, 128), F32),
        ("maskb", (128, 128), F32), ("identf", (128, 128), F32), ("identb", (128, 128), BF16),
        ("ones", (128, 128), F32), ("sel", (16, 2048), F32), ("rt", (128, 128), F32)]}
    gT = nc.dram_tensor("gT", [4, 128, NCH * 128], BF16, kind="ExternalOutput").ap()
    sc_xs = nc.dram_tensor("sc_xs", [NCH, 128, 512], F32, kind="Internal").ap()
    sc_B = nc.dram_tensor("sc_B", [NCH, 128, 128], BF16, kind="Internal").ap()
    sc_CT = nc.dram_tensor("sc_CT", [NCH, 128, 128], BF16, kind="Internal").ap()
    sc_cbb = nc.dram_tensor("sc_cbb", [NCH, 128, 128], F32, kind="Internal").ap()
    sc_yf = nc.dram_tensor("sc_yf", [NCH, 128, 512], F32, kind="Internal").ap()

    with contextlib.ExitStack() as st:
        def sb(name, shape, dt):
            return st.enter_context(nc.sbuf_tensor(name, shape, dt))

        def ps(name, shape, dt):
            return st.enter_context(nc.psum_tensor(name, shape, dt))

        P = Prog(nc)
        K = KB(nc, P)
        Wb = sb("Wb", [128, 16, 1296], BF16)
        bW = Buf("Wb")
        K.dma("gpsimd", Wb[:], w.rearrange("(kc p) n -> p kc n", p=128), writes=[bW])
        csb = {}
        bC = Buf("consts")
        for n in cn:
            shp = [16, 2048] if n == "sel" else [128, 128]
            csb[n] = sb("c_" + n, shp, BF16 if n == "identb" else F32)
            K.dma("sync", csb[n][:], cn[n], writes=[bC])
        cw = sb("cw", [128, 6, 5], F32)
        cbias = sb("cbias", [128, 6], F32)
        K.dma("sync", cw[:], convw, writes=[bC])
        K.dma("sync", cbias[:], convb, writes=[bC])
        hvb = sb("hvb", [128, 3, 16], F32)
        K.dma("sync", hvb[:], hv.partition_broadcast(128), writes=[bC])
        nwb = sb("nwb", [128, 512], F32)
        K.dma("sync", nwb[:], normw.partition_broadcast(128), writes=[bC])
        a_bc = sb("a_bc", [128, 16], F32)
        dsum = sb("dsum", [128, 8], F32)
        K.act(a_bc[:], hvb[:, 1, :], AF.Exp, [bC], [bC])
        K.ts("vector", a_bc[:], a_bc[:], -1.0, None, ALU.mult, None, [bC], [bC])
        K.tt("vector", dsum[:], hvb[:, 2, 0:8], hvb[:, 2, 8:16], ALU.add, [bC], [bC])

        pb = [ps("pb%d" % i, [128, 512], F32) for i in range(7)]
        pbf = ps("pbf", [128, 1024], BF16)
        bP = [Buf("pb%d" % i) for i in range(7)]
        bPbf = Buf("pbf")

        ut = [sb("ut%d" % i, [128, 16, 132], BF16) for i in range(2)]
        bUt = [Buf("ut%d" % i) for i in range(2)]
        NJ = NCH * 16
        dtraw = sb("dtraw", [128, NCH, 16], F32)
        dtv = sb("dtv", [128, NCH, 16], F32)
        da = sb("da", [128, NCH, 16], F32)
        cs = sb("cs", [128, NCH, 16], F32)
        tot = sb("tot", [128, NCH, 16], F32)
        dchunk = sb("dchunk", [128, NCH, 16], F32)
        dece = sb("dece", [128, NCH, 16], F32)
        ecs = sb("ecs", [128, NCH, 16], F32)
        negcs = sb("negcs", [128, NCH, 16], F32)
        csT = sb("csT", [16, NCH * 128], F32)
        bD = Buf("dstuff")
        for c0 in range(0, NCH, 32):
            n = min(32, NCH - c0)
            for ci in range(n):
                c = c0 + ci
                s = c % 2
                K.dma("sync", ut[s][:], uT[c], writes=[bUt[s]])
                for kc in range(16):
                    K.mm(pb[0][:, ci * 16:(ci + 1) * 16], ut[s][:, kc, 2:130], Wb[:, kc, 1280:1296],
                         kc == 0, kc == 15, [bUt[s], bW], [bP[0]])
            K.copy("scalar", dtraw[:, c0:c0 + n, :], pb[0][:, 0:n * 16].rearrange("p (c j) -> p c j", j=16),
                   [bP[0]], [bD])
        K.tt("vector", dtv[:], dtraw[:], hvb[:, 0:1, :].to_broadcast([128, NCH, 16]), ALU.add, [bD, bC], [bD])
        K.act(dtv[:], dtv[:], AF.Exp, [bD], [bD])
        K.act(dtv[:], dtv[:], AF.Ln, [bD], [bD], bias=1.0)
        K.tt("vector", da[:], dtv[:], a_bc[:].unsqueeze(1).to_broadcast([128, NCH, 16]), ALU.mult, [bD, bC], [bD])
        for c0 in range(0, NCH, 32):
            n = min(32, NCH - c0)
            for d, tri in ((0, "trif"), (1, "trib")):
                K.mm(pb[1][:, 0:n * 8].rearrange("p (c j) -> p c j", j=8), csb[tri][:], da[:, c0:c0 + n, d * 8:d * 8 + 8],
                     True, True, [bD, bC], [bP[1]])
                K.copy("scalar", cs[:, c0:c0 + n, d * 8:d * 8 + 8], pb[1][:, 0:n * 8].rearrange("p (c j) -> p c j", j=8),
                       [bP[1]], [bD])
            K.mm(pb[2][:, 0:n * 16].rearrange("p (c j) -> p c j", j=16), csb["ones"][:], da[:, c0:c0 + n, :],
                 True, True, [bD, bC], [bP[2]])
            K.copy("scalar", tot[:, c0:c0 + n, :], pb[2][:, 0:n * 16].rearrange("p (c j) -> p c j", j=16), [bP[2]], [bD])
        K.act(dchunk[:], tot[:], AF.Exp, [bD], [bD])
        K.tt("vector", dece[:], tot[:], cs[:], ALU.subtract, [bD], [bD])
        K.act(dece[:], dece[:], AF.Exp, [bD], [bD])
        K.act(ecs[:], cs[:], AF.Exp, [bD], [bD])
        K.ts("vector", negcs[:], cs[:], -1.0, None, ALU.mult, None, [bD], [bD])
        for c0 in range(0, NCH, 4):
            n = min(4, NCH - c0)
            for ci in range(n):
                K.tr(pb[3][0:16, ci * 128:(ci + 1) * 128], cs[:, c0 + ci, :], csb["identf"][:], [bD, bC], [bP[3]])
            K.copy("scalar", csT[:, c0 * 128:(c0 + n) * 128], pb[3][0:16, 0:n * 128], [bP[3]], [bD])

        acc = [sb("acc%d" % i, [128, 128], F32) for i in range(6)]
        bAcc = [Buf("acc%d" % i) for i in range(6)]
        xsT = [sb("xsT%d" % i, [128, 128], F32) for i in range(6)]
        bXsT = [Buf("xsT%d" % i) for i in range(6)]
        cos_t = sb("cos_t", [128, 128], F32)
        sin_t = sb("sin_t", [128, 128], F32)
        bCos = Buf("cos")
        rt1 = sb("rt1", [128, 2, 128], F32)
        rt2 = sb("rt2", [128, 2, 128], F32)
        bRt1, bRt2 = Buf("rt1"), Buf("rt2")
        BT = sb("BT", [128, 128], BF16)
        CT = [sb("CT%d" % i, [128, 128], BF16) for i in range(2)]
        bBT = Buf("BT")
        bCT = [Buf("CT0"), Buf("CT1")]
        xs_tm = [sb("xs_tm%d" % i, [128, 512], F32) for i in range(2)]
        bXs = [Buf("xs_tm0"), Buf("xs_tm1")]
        B_tm = [sb("B_tm%d" % i, [128, 128], BF16) for i in range(2)]
        bBtm = [Buf("B_tm0"), Buf("B_tm1")]
        cbm = [sb("cbm%d" % i, [128, 128], F32) for i in range(2)]
        bCbm = [Buf("cbm0"), Buf("cbm1")]
        cbb = [sb("cbb%d" % i, [128, 128], F32) for i in range(2)]
        bCbb = [Buf("cbb0"), Buf("cbb1")]
        xdt = sb("xdt", [128, 512], BF16)
        xw = sb("xw", [128, 512], BF16)
        bXdt, bXw = Buf("xdt"), Buf("xw")
        Esb = [sb("E%d" % i, [128, 128], F32) for i in range(2)]
        bE = [Buf("E0"), Buf("E1")]
        MT = [sb("MT%d" % i, [128, 128], BF16) for i in range(2)]
        bMT = [Buf("MT0"), Buf("MT1")]
        ytmp = sb("ytmp", [128, 512], F32)
        bYtmp = Buf("ytmp")
        yy = [sb("yy%d" % i, [128, 512], F32) for i in range(2)]
        bYy = [Buf("yy0"), Buf("yy1")]
        yfl = [sb("yfl%d" % i, [128, 512], F32) for i in range(2)]
        bYfl = [Buf("yfl0"), Buf("yfl1")]
        xsd = sb("xsd", [128, 512], F32)
        bXsd = Buf("xsd")
        hst = [sb("hst%d" % i, [128, 512], F32) for i in range(2)]
        hbf = [sb("hbf%d" % i, [128, 512], BF16) for i in range(2)]
        bH = [Buf("h0"), Buf("h1")]
        bHbf = [Buf("hbf0"), Buf("hbf1")]
        htmp = sb("htmp", [128, 512], F32)
        bHtmp = Buf("htmp")
        sz = sb("sz", [128, 512], F32)
        gg = sb("gg", [128, 512], F32)
        gsq = sb("gsq", [128, 512], F32)
        ssum = sb("ssum", [128, 1], F32)
        rstd = sb("rstd", [128, 1], F32)
        gn = sb("gn", [128, 512], BF16)
        gTs = [sb("gTs%d" % i, [128, 4, 128], BF16) for i in range(2)]
        bSz, bGg, bGsq, bSs, bRstd, bGn = Buf("sz"), Buf("gg"), Buf("gsq"), Buf("ss"), Buf("rstd"), Buf("gn")
        bGTs = [Buf("gTs0"), Buf("gTs1")]
        for d in range(2):
            K.memset("vector", hst[d][:], 0.0, [bH[d]])
            K.memset("vector", hbf[d][:], 0.0, [bHbf[d]])

        def bc8(t, c, j0):
            return t[:, c, j0:j0 + 8].unsqueeze(2).to_broadcast([128, 8, 64])

        def v3(ap):
            return ap.rearrange("p (h d) -> p h d", h=8)

        def scan_step(c, d, xs_ap, bxs, Btm_ap, bbtm, CT_ap, bct, cbm_ap, bcbm, y_out, by_out):
            j0 = d * 8
            K.tt("gpsimd", v3(xdt[:]), v3(xs_ap), bc8(dtv, c, j0), ALU.mult, [bxs, bD], [bXdt])
            K.tt("gpsimd", v3(xw[:]), v3(xdt[:]), bc8(dece, c, j0), ALU.mult, [bXdt, bD], [bXw])
            for h in range(8):
                j = j0 + h
                e = h % 2
                pcs = pb[3 + (h // 4)]
                bpcs = bP[3 + (h // 4)]
                reg = pcs[:, (h % 4) * 128:(h % 4 + 1) * 128]
                K.mm(reg, csb["sel"][:, j * 128:(j + 1) * 128], csT[:, c * 128:(c + 1) * 128], True, True,
                     [bC, bD], [bpcs])
                K.act(Esb[e][:], reg, AF.Exp, [bpcs, bD], [bE[e]], bias=negcs[:, c, j:j + 1])
                K.stt("vector", MT[e][:], Esb[e][:], 1.0, cbm_ap, ALU.min, ALU.mult, [bE[e], bcbm], [bMT[e]])
                K.mm(pb[5][:, h * 64:(h + 1) * 64], MT[e][:], xdt[:, h * 64:(h + 1) * 64], True, True,
                     [bMT[e], bXdt], [bP[5]])
            K.mm(pb[6][:], CT_ap, hbf[d][:], True, True, [bct, bHbf[d]], [bP[6]])
            K.mm(pb[2][:], Btm_ap, xw[:], True, True, [bbtm, bXw], [bP[2]])
            K.tt("vector", v3(ytmp[:]), v3(pb[6][:]), bc8(ecs, c, j0), ALU.mult, [bP[6], bD], [bYtmp])
            K.tt("vector", y_out, ytmp[:], pb[5][:], ALU.add, [bYtmp, bP[5]], [by_out])
            K.tt("gpsimd", v3(htmp[:]), v3(hst[d][:]), bc8(dchunk, c, j0), ALU.mult, [bH[d], bD], [bHtmp])
            K.tt("vector", hst[d][:], htmp[:], pb[2][:], ALU.add, [bHtmp, bP[2]], [bH[d]])
            K.copy("scalar", hbf[d][:], hst[d][:], [bH[d]], [bHbf[d]])

        bSc = {n: [Buf('%s%d' % (n, c)) for c in range(NCH)] for n in ('xs', 'B', 'CT', 'cbb', 'yf')}
        for c in range(NCH):
            s = c % 2
            lat = c >= NCTX
            K.dma("sync", ut[s][:], uT[c], writes=[bUt[s]])
            if lat:
                lc = c - NCTX
                K.dma("sync", cos_t[:], cosd[:, lc * 128:(lc + 1) * 128], writes=[bCos])
                K.dma("sync", sin_t[:], sind[:, lc * 128:(lc + 1) * 128], writes=[bCos])
            for blk in range(6):
                reg = pb[blk // 3][:, (blk % 3) * 132:(blk % 3 + 1) * 132]
                for kc in range(16):
                    K.mm(reg, Wb[:, kc, 512 + blk * 128:512 + (blk + 1) * 128], ut[s][:, kc, :], kc == 0, kc == 15,
                         [bW, bUt[s]], [bP[blk // 3]])
            for j in range(5):
                for blk in range(6):
                    reg = pb[blk // 3][:, (blk % 3) * 132 + j:(blk % 3) * 132 + j + 128]
                    if j == 0:
                        K.ts("vector", acc[blk][:], reg, cw[:, blk, 0:1], None, ALU.mult, None,
                             [bP[blk // 3], bC], [bAcc[blk]])
                    else:
                        K.stt("vector", acc[blk][:], reg, cw[:, blk, j:j + 1], acc[blk][:], ALU.mult, ALU.add,
                              [bP[blk // 3], bC, bAcc[blk]], [bAcc[blk]])
            for blk in range(6):
                K.act(xsT[blk][:], acc[blk][:], AF.Silu, [bAcc[blk], bC], [bXsT[blk]], bias=cbias[:, blk:blk + 1])
            if lat:
                K.mm(pb[1][:, 0:128], csb["rt"][:], xsT[4][:], True, True, [bC, bXsT[4]], [bP[1]])
                K.mm(pb[1][:, 128:256], csb["rt"][:], xsT[5][:], True, True, [bC, bXsT[5]], [bP[1]])
                for i, blk in enumerate((4, 5)):
                    K.tt("gpsimd", rt1[:, i, :], xsT[blk][:], cos_t[:], ALU.mult, [bXsT[blk], bCos], [bRt1])
                K.tt("vector", rt2[:], pb[1][:, 0:256].rearrange("p (i n) -> p i n", i=2),
                     sin_t[:].unsqueeze(1).to_broadcast([128, 2, 128]), ALU.mult, [bP[1], bCos], [bRt2])
                K.tt("gpsimd", BT[:], rt1[:, 0, :], rt2[:, 0, :], ALU.add, [bRt1, bRt2], [bBT])
                K.tt("gpsimd", CT[s][:], rt1[:, 1, :], rt2[:, 1, :], ALU.add, [bRt1, bRt2], [bCT[s]])
            else:
                K.copy("scalar", BT[:], xsT[4][:], [bXsT[4]], [bBT])
                K.copy("scalar", CT[s][:], xsT[5][:], [bXsT[5]], [bCT[s]])
            for blk in range(4):
                K.tr(pb[0][:, blk * 128:(blk + 1) * 128], xsT[blk][:], csb["identf"][:], [bXsT[blk], bC], [bP[0]])
            K.copy("scalar", xs_tm[s][:], pb[0][:], [bP[0]], [bXs[s]])
            K.tr(pbf[:, 0:128], BT[:], csb["identb"][:], [bBT, bC], [bPbf])
            K.copy("scalar", B_tm[s][:], pbf[:, 0:128], [bPbf], [bBtm[s]])
            K.mm(pb[1][:, 256:384], BT[:], CT[s][:], True, True, [bBT, bCT[s]], [bP[1]])
            K.tt("vector", cbm[s][:], pb[1][:, 256:384], csb["maskf"][:], ALU.mult, [bP[1], bC], [bCbm[s]])
            K.tt("vector", cbb[s][:], pb[1][:, 256:384], csb["maskb"][:], ALU.mult, [bP[1], bC], [bCbb[s]])
            K.dma("sync", sc_xs[c], xs_tm[s][:], reads=[bXs[s]], writes=[bSc['xs'][c]])
            K.dma("sync", sc_B[c], B_tm[s][:], reads=[bBtm[s]], writes=[bSc['B'][c]])
            K.dma("sync", sc_CT[c], CT[s][:], reads=[bCT[s]], writes=[bSc['CT'][c]])
            K.dma("sync", sc_cbb[c], cbb[s][:], reads=[bCbb[s]], writes=[bSc['cbb'][c]])
            scan_step(c, 0, xs_tm[s][:], bXs[s], B_tm[s][:], bBtm[s], CT[s][:], bCT[s], cbm[s][:], bCbm[s],
                      yy[s][:], bYy[s])
            K.tt("gpsimd", v3(xsd[:]), v3(xs_tm[s][:]), dsum[:].unsqueeze(2).to_broadcast([128, 8, 64]), ALU.mult,
                 [bXs[s], bC], [bXsd])
            K.tt("gpsimd", yy[s][:], yy[s][:], xsd[:], ALU.add, [bYy[s], bXsd], [bYy[s]])
            K.dma("sync", sc_yf[c], yy[s][:], reads=[bYy[s]], writes=[bSc['yf'][c]])

        order = list(range(NCTX - 1, -1, -1)) + list(range(NCH - 1, NCTX - 1, -1))
        for i, c in enumerate(order):
            s = i % 2
            K.dma("sync", ut[s][:], uT[c], writes=[bUt[s]])
            K.dma("sync", xs_tm[s][:], sc_xs[c], reads=[bSc['xs'][c]], writes=[bXs[s]])
            K.dma("sync", B_tm[s][:], sc_B[c], reads=[bSc['B'][c]], writes=[bBtm[s]])
            K.dma("sync", CT[s][:], sc_CT[c], reads=[bSc['CT'][c]], writes=[bCT[s]])
            K.dma("sync", cbb[s][:], sc_cbb[c], reads=[bSc['cbb'][c]], writes=[bCbb[s]])
            K.dma("sync", yfl[s][:], sc_yf[c], reads=[bSc['yf'][c]], writes=[bYfl[s]])
            scan_step(c, 1, xs_tm[s][:], bXs[s], B_tm[s][:], bBtm[s], CT[s][:], bCT[s], cbb[s][:], bCbb[s],
                      yy[s][:], bYy[s])
            K.tt("gpsimd", yy[s][:], yy[s][:], yfl[s][:], ALU.add, [bYy[s], bYfl[s]], [bYy[s]])
            for kc in range(16):
                K.mm(pb[0][:], ut[s][:, kc, 2:130], Wb[:, kc, 0:512], kc == 0, kc == 15, [bUt[s], bW], [bP[0]])
            K.act(sz[:], pb[0][:], AF.Silu, [bP[0]], [bSz])
            K.tt("vector", gg[:], yy[s][:], sz[:], ALU.mult, [bYy[s], bSz], [bGg])
            K.act(gsq[:], gg[:], AF.Square, [bGg], [bGsq])
            K.rsum("vector", ssum[:], gsq[:], [bGsq], [bSs])
            K.act(rstd[:], ssum[:], AF.Sqrt, [bSs], [bRstd], bias=EPS, scale=1.0 / 512.0)
            K.P.add("vector", lambda e: e.reciprocal(out=rstd[:], in_=rstd[:]), [bRstd], [bRstd])
            K.stt("vector", gn[:], gg[:], rstd[:, 0:1], nwb[:], ALU.mult, ALU.mult, [bGg, bRstd, bC], [bGn])
            for blk in range(4):
                K.tr(pbf[:, 128 + blk * 128:128 + (blk + 1) * 128], gn[:, blk * 128:(blk + 1) * 128], csb["identb"][:],
                     [bGn, bC], [bPbf])
            K.copy("scalar", gTs[s][:], pbf[:, 128:640].rearrange("p (b t) -> p b t", b=4), [bPbf], [bGTs[s]])
            K.dma("sync", gT[:, :, c * 128:(c + 1) * 128].rearrange("b p t -> p b t"), gTs[s][:], reads=[bGTs[s]])
        P.emit()
    return nc

import contextlib
import numpy as np
import ml_dtypes
import concourse.bass as bass
import concourse.mybir as mybir

NEG = -30000.0


def na_class(rp, NLAT):
    if rp == 0:
        return 0
    if rp == 1:
        return 1
    if rp == NLAT - 2:
        return 3
    if rp == NLAT - 1:
        return 4
    return 2


def na_ws(rp, NLAT):
    return int(np.clip(2 * rp - 4, 0, 2 * NLAT - 8))


def na_bias(rpb_h, NLAT):
    out = np.full((5, 128, 832), NEG, np.float32)
    out[:, :, 576:] = 0.0
    rows = 2 * NLAT
    reps = {0: 0, 1: 1, 2: 2, 3: NLAT - 2, 4: NLAT - 1}
    qc = np.arange(64)
    cstart = np.clip(qc - 8, 0, 48)
    for cls, rp in reps.items():
        ws = na_ws(rp, NLAT)
        for qr in range(2):
            r = 2 * rp + qr
            rs = int(np.clip(r - 4, 0, rows - 8))
            for i in range(9):
                kr = ws + i
                if kr < rs or kr >= rs + 8 or kr >= rows:
                    continue
                drow = kr - r + 7
                for q in range(64):
                    k0 = cstart[q]
                    kcs = np.arange(k0, k0 + 16)
                    out[cls, qr * 64 + q, i * 64 + kcs] = rpb_h[drow, kcs - q + 15]
    return out


def build_na(NCTX=2, NLAT=64, ctx_out=True):
    NCH = NCTX + NLAT
    TL = NLAT * 128
    nc = bass.Bass("TRN2", target_bir_lowering=False)
    din = lambda name, shape, dt: nc.dram_tensor(name, shape, dt, kind="ExternalInput").ap()
    uT = din("uT", [NCH, 128, 16, 132], BF16)
    w = din("wna", [2048, 768], F32)
    biasd = din("nabias", [2, 5, 128, 832], F32)
    identb_d = din("identb", [128, 128], BF16)
    oT = nc.dram_tensor("oT", [2, 128, NCH * 128], BF16, kind="ExternalOutput").ap()

    with contextlib.ExitStack() as st:
        def sb(name, shape, dt):
            return st.enter_context(nc.sbuf_tensor(name, shape, dt))

        def ps(name, shape, dt):
            return st.enter_context(nc.psum_tensor(name, shape, dt))

        P = Prog(nc)
        K = KB(nc, P)
        Wb = sb("Wb_s", [128, 16, 768], BF16)
        bW = Buf("Wb")
        K.dma("gpsimd", Wb[:], w.rearrange("(kc p) n -> p kc n", p=128), writes=[bW])
        identb = sb("identb_s", [128, 128], BF16)
        bC = Buf("c")
        K.dma("sync", identb[:], identb_d, writes=[bC])
        ut = [sb("ut%d" % i, [128, 16, 132], BF16) for i in range(2)]
        bUt = [Buf("ut%d" % i) for i in range(2)]
        QT = sb("QT", [128, NCH * 128], BF16)
        KT = sb("KT", [128, NCH * 128 + 64], BF16)
        Vt = sb("Vt", [128, NCH + 1, 128], BF16)
        bQ, bK, bV = Buf("QT"), Buf("KT"), Buf("Vt")
        bias = sb("bias", [128, 5, 832], F32)
        bB = Buf("bias")
        pq = [ps("pq%d" % i, [128, 512], F32) for i in range(4)]
        bPq = [Buf("pq%d" % i) for i in range(4)]
        pA = ps("pA", [128, 512], F32)
        pB = ps("pB", [128, 512], F32)
        pO = ps("pO", [128, 512], F32)
        pT = ps("pT", [128, 1024], BF16)
        bPA, bPB, bPO, bPT = Buf("pA"), Buf("pB"), Buf("pO"), Buf("pT")
        S_sb = [sb("S%d" % i, [128, 832], F32) for i in range(2)]
        bS = [Buf("S0"), Buf("S1")]
        Pf = sb("Pf", [128, 832], F32)
        bPf = Buf("Pf")
        Pn = sb("Pn", [128, 832], BF16)
        bPn = Buf("Pn")
        PT = sb("PT", [128, 7, 128], BF16)
        bPTs = Buf("PTs")
        mx = sb("mx", [128, 1], F32)
        nmx = sb("nmx", [128, 1], F32)
        rs = sb("rs", [128, 1], F32)
        rinv = sb("rinv", [128, 1], F32)
        bMx, bNmx, bRs, bRinv = Buf("mx"), Buf("nmx"), Buf("rs"), Buf("rinv")
        oTs = [sb("oTs%d" % i, [128, 128], BF16) for i in range(2)]
        bOT = [Buf("oTs0"), Buf("oTs1")]
        SCALE = 128.0 ** -0.5
        K.memset("vector", KT[:, NCH * 128:NCH * 128 + 64], 0.0, [bK])
        K.memset("vector", Vt[:, NCH, :], 0.0, [bV])

        def softmax_pv(nk, nchunks_k, kparts, vparts, q_chunk, out_dram, slot):
            K.P.add("vector", lambda e: e.reduce_max(out=mx[:], in_=S_sb[slot][:, 0:nk], axis=AX.X), [bS[slot]], [bMx])
            K.ts("vector", nmx[:], mx[:], -1.0, None, ALU.mult, None, [bMx], [bNmx])
            K.act(Pf[:, 0:nk], S_sb[slot][:, 0:nk], AF.Exp, [bS[slot], bNmx], [bPf], bias=nmx[:, 0:1])
            K.rsum("vector", rs[:], Pf[:, 0:nk], [bPf], [bRs])
            K.P.add("vector", lambda e: e.reciprocal(out=rinv[:], in_=rs[:]), [bRs], [bRinv])
            K.ts("vector", Pn[:, 0:nk], Pf[:, 0:nk], rinv[:, 0:1], None, ALU.mult, None, [bPf, bRinv], [bPn])
            for i, (c0, ncol) in enumerate(kparts):
                K.tr(pT[0:ncol, i * 128:(i + 1) * 128], Pn[:, c0:c0 + ncol], identb[:], [bPn, bC], [bPT])
            n = len(kparts)
            K.copy("scalar", PT[:, 0:n, :], pT[:, 0:n * 128].rearrange("p (c q) -> p c q", q=128), [bPT], [bPTs])
            for i, (vap, kk) in enumerate(vparts):
                K.mm(pO[:, 0:128], vap, PT[0:kk, i, :], i == 0, i == n - 1, [bV, bPTs], [bPO])
            K.copy("scalar", oTs[slot][:], pO[:, 0:128], [bPO], [bOT[slot]])
            K.dma("sync", out_dram, oTs[slot][:], reads=[bOT[slot]])

        for hd in range(2):
            K.dma("sync", bias[:], biasd[hd].rearrange("c p k -> p c k"), writes=[bB])
            for c in range(NCH):
                s = c % 2
                K.dma("sync", ut[s][:], uT[c], writes=[bUt[s]])
                for i, (dst, bdst) in enumerate(((QT, bQ), (KT, bK))):
                    pp = pq[i]
                    for kc in range(16):
                        K.mm(pp[:, 0:128], Wb[:, kc, hd * 384 + i * 128:hd * 384 + (i + 1) * 128], ut[s][:, kc, 2:130],
                             kc == 0, kc == 15, [bW, bUt[s]], [bPq[i]])
                    if i == 0:
                        K.act(dst[:, c * 128:(c + 1) * 128], pp[:, 0:128], AF.Copy, [bPq[i]], [bdst], scale=SCALE)
                    else:
                        K.copy("vector", dst[:, c * 128:(c + 1) * 128], pp[:, 0:128], [bPq[i]], [bdst])
                for kc in range(16):
                    K.mm(pq[2][:, 0:128], ut[s][:, kc, 2:130], Wb[:, kc, hd * 384 + 256:hd * 384 + 384],
                         kc == 0, kc == 15, [bW, bUt[s]], [bPq[2]])
                K.copy("scalar", Vt[:, c, :], pq[2][:, 0:128], [bPq[2]], [bV])
            KTc = KT[:, 0:NCTX * 128]
            for rp in range(NLAT):
                slot = rp % 2
                ws = na_ws(rp, NLAT)
                cls = na_class(rp, NLAT)
                qc = NCTX + rp
                k0 = NCTX * 128 + 64 * ws
                qap = QT[:, qc * 128:(qc + 1) * 128]
                K.mm(pA[:], qap, KT[:, k0:k0 + 512], True, True, [bQ, bK], [bPA])
                K.mm(pB[:, 0:64], qap, KT[:, k0 + 512:k0 + 576], True, True, [bQ, bK], [bPB])
                K.mm(pB[:, 64:64 + NCTX * 128], qap, KTc, True, True, [bQ, bK], [bPB])
                K.tt("vector", S_sb[slot][:, 0:512], pA[:], bias[:, cls, 0:512], ALU.add, [bPA, bB], [bS[slot]])
                K.tt("vector", S_sb[slot][:, 512:832], pB[:, 0:320], bias[:, cls, 512:832], ALU.add, [bPB, bB], [bS[slot]])
                kparts = [(i * 128, 128) for i in range(4)] + [(512, 64)] + [(576 + i * 128, 128) for i in range(NCTX)]
                vc0 = NCTX + ws // 2
                vparts = [(Vt[:, vc0 + i, :], 128) for i in range(4)] + [(Vt[0:64, vc0 + 4, :], 64)] + \
                         [(Vt[:, i, :], 128) for i in range(NCTX)]
                softmax_pv(832, 7, kparts, vparts, qc, oT[hd][:, qc * 128:(qc + 1) * 128], slot)
            if ctx_out:
                for cq in range(NCTX):
                    slot = cq % 2
                    qap = QT[:, cq * 128:(cq + 1) * 128]
                    K.mm(pA[:, 0:NCTX * 128], qap, KTc, True, True, [bQ, bK], [bPA])
                    K.copy("vector", S_sb[slot][:, 0:NCTX * 128], pA[:, 0:NCTX * 128], [bPA], [bS[slot]])
                    kparts = [(i * 128, 128) for i in range(NCTX)]
                    vparts = [(Vt[:, i, :], 128) for i in range(NCTX)]
                    softmax_pv(NCTX * 128, NCTX, kparts, vparts, cq, oT[hd][:, cq * 128:(cq + 1) * 128], slot)
        P.emit()
    return nc

import contextlib
import numpy as np
import ml_dtypes
import concourse.bass as bass
import concourse.mybir as mybir

EPS = 1e-6
TP = 352
NPASS = 3
TT = TP * NPASS
NLATC = 1024


def build_mod(NL=2):
    nc = bass.Bass("TRN2", target_bir_lowering=False)
    din = lambda name, shape, dt: nc.dram_tensor(name, shape, dt, kind="ExternalInput").ap()
    c2 = din("c2", [128, 16, 2], F32)
    wa = din("wada", [NL, 2048, 1536], F32)
    ba = din("bada", [NL, 1536], F32)
    out = nc.dram_tensor("mod", [NL, 2, 1536], F32, kind="ExternalOutput").ap()
    with contextlib.ExitStack() as st:
        sb = lambda name, shape, dt: st.enter_context(nc.sbuf_tensor(name, shape, dt))
        ps = lambda name, shape, dt: st.enter_context(nc.psum_tensor(name, shape, dt))
        P = Prog(nc)
        K = KB(nc, P)
        cf = sb("cf", [128, 16, 2], F32)
        cb = sb("cb", [128, 16, 2], BF16)
        bC = Buf("c")
        K.dma("sync", cf[:], c2, writes=[bC])
        K.act(cb[:], cf[:], AF.Silu, [bC], [bC])
        Wt = [sb("W%d" % i, [128, 16, 512], BF16) for i in range(2)]
        bW = [Buf("W0"), Buf("W1")]
        bb = [sb("bb%d" % i, [2, 512], F32) for i in range(2)]
        bBb = [Buf("bb0"), Buf("bb1")]
        pp = [ps("pp%d" % i, [128, 512], F32) for i in range(2)]
        bP = [Buf("pp0"), Buf("pp1")]
        ob = [sb("ob%d" % i, [2, 512], F32) for i in range(2)]
        bO = [Buf("ob0"), Buf("ob1")]
        i = 0
        for l in range(NL):
            for ct in range(3):
                s = i % 2
                i += 1
                K.dma("gpsimd", Wt[s][:], wa[l][:, ct * 512:(ct + 1) * 512].rearrange("(kc p) n -> p kc n", p=128), writes=[bW[s]])
                K.dma("sync", bb[s][:], ba[l][ct * 512:(ct + 1) * 512].partition_broadcast(2), writes=[bBb[s]])
                for kc in range(16):
                    K.mm(pp[s][0:2, :], cb[:, kc, :], Wt[s][:, kc, :], kc == 0, kc == 15, [bC, bW[s]], [bP[s]])
                K.tt("vector", ob[s][:], pp[s][0:2, :], bb[s][:], ALU.add, [bP[s], bBb[s]], [bO[s]])
                K.dma("sync", out[l][:, ct * 512:(ct + 1) * 512], ob[s][:], reads=[bO[s]])
        P.emit()
    return nc


class DenseCtx:
    def __init__(self, nc, st, K):
        self.nc, self.st, self.K = nc, st, K
        self.sb = lambda name, shape, dt: st.enter_context(nc.sbuf_tensor(name, shape, dt))
        self.ps = lambda name, shape, dt: st.enter_context(nc.psum_tensor(name, shape, dt))


def rms_stats(K, D, src_blocks, bsrc, T, ones, bC, sq, bSq, pst, bPst, rstd_bc, bR, nfeat=2048.0):
    nb = len(src_blocks)
    for b, ap in enumerate(src_blocks):
        s = b % 2
        K.act(sq[s][:, 0:T], ap, AF.Square, [bsrc], [bSq[s]])
        K.mm(pst[:, 0:T], ones[:], sq[s][:, 0:T], b == 0, b == nb - 1, [bC, bSq[s]], [bPst])
    K.act(rstd_bc[:, 0:T], pst[:, 0:T], AF.Sqrt, [bPst], [bR], bias=EPS, scale=1.0 / nfeat)
    K.P.add("vector", lambda e: e.reciprocal(out=rstd_bc[:, 0:T], in_=rstd_bc[:, 0:T]), [bR], [bR])


def tok_ranges(p):
    g0, g1 = p * TP, (p + 1) * TP
    out = []
    if g0 < NLATC:
        out.append((0, min(g1, NLATC) - g0, 0))
    if g1 > NLATC:
        out.append((max(g0, NLATC) - g0, TP, 1))
    return out


def build_normA():
    nc = bass.Bass("TRN2", target_bir_lowering=False)
    din = lambda name, shape, dt: nc.dram_tensor(name, shape, dt, kind="ExternalInput").ap()
    xT = din("xT", [16, 128, TT], F32)
    modv = din("modv", [128, 6, 2, 16], F32)
    gains = din("gains", [128, 4, 16], F32)
    onesd = din("ones", [128, 128], F32)
    uT = nc.dram_tensor("uTo", [16, 128, TT], BF16, kind="ExternalOutput").ap()
    with contextlib.ExitStack() as st:
        P = Prog(nc)
        K = KB(nc, P)
        D = DenseCtx(nc, st, K)
        bC = Buf("c")
        ones = D.sb("ones_s", [128, 128], F32)
        mv = D.sb("mv", [128, 6, 2, 16], F32)
        gn = D.sb("gn", [128, 4, 16], F32)
        K.dma("sync", ones[:], onesd, writes=[bC])
        K.dma("sync", mv[:], modv, writes=[bC])
        K.dma("sync", gn[:], gains, writes=[bC])
        cA = D.sb("cA", [128, 2, 16], F32)
        K.ts("vector", cA[:], mv[:, 1, :, :], 1.0, None, ALU.add, None, [bC], [bC])
        K.tt("vector", cA[:], cA[:], gn[:, 0:1, :].to_broadcast([128, 2, 16]), ALU.mult, [bC], [bC])
        xs = [D.sb("xs%d" % i, [128, 16, TP], F32) for i in range(2)]
        bX = [Buf("x0"), Buf("x1")]
        us = [D.sb("us%d" % i, [128, 16, TP], BF16) for i in range(2)]
        bU = [Buf("u0"), Buf("u1")]
        sq = [D.sb("sq%d" % i, [128, TP], F32) for i in range(2)]
        bSq = [Buf("sq0"), Buf("sq1")]
        pst = D.ps("pst", [128, 512], F32)
        bPst = Buf("pst")
        rstd = D.sb("rstd", [128, TP], F32)
        bR = Buf("rstd")
        tmp = [D.sb("tmp%d" % i, [128, TP], F32) for i in range(2)]
        bT = [Buf("t0"), Buf("t1")]
        for p in range(NPASS):
            s = p % 2
            K.dma("sync", xs[s][:], xT[:, :, p * TP:(p + 1) * TP].rearrange("b p t -> p b t"), writes=[bX[s]])
            rms_stats(K, D, [xs[s][:, b, :] for b in range(16)], bX[s], TP, ones, bC, sq, bSq, pst, bPst, rstd, bR)
            for b in range(16):
                t = b % 2
                K.tt("vector", tmp[t][:], xs[s][:, b, :], rstd[:], ALU.mult, [bX[s], bR], [bT[t]])
                for lo, hi, kind in tok_ranges(p):
                    K.ts("gpsimd", us[s][:, b, lo:hi], tmp[t][:, lo:hi], cA[:, kind, b:b + 1], mv[:, 0, kind, b:b + 1],
                         ALU.mult, ALU.add, [bT[t], bC], [bU[s]])
            K.dma("sync", uT[:, :, p * TP:(p + 1) * TP].rearrange("b p t -> p b t"), us[s][:], reads=[bU[s]])
        P.emit()
    return nc


def build_dense():
    nc = bass.Bass("TRN2", target_bir_lowering=False)
    din = lambda name, shape, dt: nc.dram_tensor(name, shape, dt, kind="ExternalInput").ap()
    gTd = din("gT", [32, 128, TT], BF16)
    oTd = din("oT", [16, 128, TT], BF16)
    uTd = din("uT", [16, 128, TT], BF16)
    xTd = din("xT", [16, 128, TT], F32)
    modv = din("modv", [128, 6, 2, 16], F32)
    gains = din("gains", [128, 4, 16], F32)
    onesd = din("ones", [128, 128], F32)
    wg = din("wg", [2048, 4096], F32)
    wsso = din("wsso", [4096, 2048], F32)
    wnao = din("wnao", [2048, 2048], F32)
    wout = din("wout", [2048, 2048], F32)
    w1 = din("w1", [2048, 8192], F32)
    w2 = din("w2", [8192, 2048], F32)
    xo = nc.dram_tensor("xTo", [16, 128, TT], F32, kind="ExternalOutput").ap()
    with contextlib.ExitStack() as st:
        P = Prog(nc)
        K = KB(nc, P)
        D = DenseCtx(nc, st, K)
        bC = Buf("c")
        ones = D.sb("ones_s", [128, 128], F32)
        mv = D.sb("mv", [128, 6, 2, 16], F32)
        gn = D.sb("gn", [128, 4, 16], F32)
        K.dma("sync", ones[:], onesd, writes=[bC])
        K.dma("sync", mv[:], modv, writes=[bC])
        K.dma("sync", gn[:], gains, writes=[bC])
        c1 = D.sb("c1", [128, 2, 16], F32)
        c2 = D.sb("c2", [128, 2, 16], F32)
        c3 = D.sb("c3", [128, 2, 16], F32)
        K.tt("vector", c1[:], mv[:, 2, :, :], gn[:, 1:2, :].to_broadcast([128, 2, 16]), ALU.mult, [bC], [bC])
        K.ts("vector", c2[:], mv[:, 4, :, :], 1.0, None, ALU.add, None, [bC], [bC])
        K.tt("vector", c2[:], c2[:], gn[:, 2:3, :].to_broadcast([128, 2, 16]), ALU.mult, [bC], [bC])
        K.tt("vector", c3[:], mv[:, 5, :, :], gn[:, 3:4, :].to_broadcast([128, 2, 16]), ALU.mult, [bC], [bC])

        arena = D.sb("arena", [128, 64, TP], BF16)
        bAr = Buf("arena")
        mh = D.sb("mh", [128, 16, TP], BF16)
        bMh = Buf("mh")
        fb = D.sb("fbuf", [128, 16, TP], F32)
        bFb = Buf("fbuf")
        xs = D.sb("xs", [128, 16, TP], F32)
        bX = Buf("xs")
        sq = [D.sb("sq%d" % i, [128, TP], F32) for i in range(2)]
        bSq = [Buf("sq0"), Buf("sq1")]
        rstd = D.sb("rstd", [128, TP], F32)
        bR = Buf("rstd")
        tmp = [D.sb("tmp%d" % i, [128, TP], F32) for i in range(2)]
        bT = [Buf("t0"), Buf("t1")]
        sg = [D.sb("sg%d" % i, [128, TP], F32) for i in range(2)]
        bSg = [Buf("sg0"), Buf("sg1")]
        m1 = [D.sb("m1%d" % i, [128, TP], F32) for i in range(2)]
        bM1 = [Buf("m10"), Buf("m11")]
        m2 = [D.sb("m2%d" % i, [128, TP], F32) for i in range(2)]
        bM2 = [Buf("m20"), Buf("m21")]
        WS = 10240
        wsl = [D.sb("wsl%d" % i, [128, WS], BF16) for i in range(2)]
        bWs = [Buf("w0"), Buf("w1")]
        pb = [D.ps("pb%d" % i, [128, 512], F32) for i in range(8)]
        bP = [Buf("pb%d" % i) for i in range(8)]
        wctr = [0]

        def wslot():
            s = wctr[0] % 2
            wctr[0] += 1
            return s

        def load_w(s, off, wd, k0, nk, c0, ncol):
            view = wsl[s][:, off:off + nk * ncol].rearrange("p (k n) -> p k n", n=ncol)
            K.dma("gpsimd", view, wd[k0 * 128:(k0 + nk) * 128, c0:c0 + ncol].rearrange("(k p) n -> p k n", p=128),
                  writes=[bWs[s]])
            return view

        def mm_acc(pbi, wview, nk, col0, in_blocks, bin_, s):
            for k in range(nk):
                K.mm(pb[pbi][:, 0:TP], wview[:, k, col0:col0 + 128], in_blocks[k], k == 0, k == nk - 1,
                     [bWs[s], bin_], [bP[pbi]])

        for p in range(NPASS):
            tsl = slice(p * TP, (p + 1) * TP)
            gT = arena[:, 0:32, :]
            oT = arena[:, 32:48, :]
            uT = arena[:, 48:64, :]
            K.dma("sync", gT, gTd[:, :, tsl].rearrange("b p t -> p b t"), writes=[bAr])
            K.dma("sync", oT, oTd[:, :, tsl].rearrange("b p t -> p b t"), writes=[bAr])
            K.dma("sync", uT, uTd[:, :, tsl].rearrange("b p t -> p b t"), writes=[bAr])
            K.dma("sync", xs[:], xTd[:, :, tsl].rearrange("b p t -> p b t"), writes=[bX])
            for g in range(16):
                s = wslot()
                vs = load_w(s, 0, wsso, 0, 32, g * 128, 128)
                va = load_w(s, 4096, wg, 0, 16, g * 128, 128)
                vn = load_w(s, 6144, wnao, 0, 16, g * 128, 128)
                vb = load_w(s, 8192, wg, 0, 16, 2048 + g * 128, 128)
                cb = g
                q = (cb % 2) * 4
                e = cb % 2
                mm_acc(q + 0, vs, 32, 0, [gT[:, k, :] for k in range(32)], bAr, s)
                mm_acc(q + 1, va, 16, 0, [uT[:, k, :] for k in range(16)], bAr, s)
                mm_acc(q + 2, vn, 16, 0, [oT[:, k, :] for k in range(16)], bAr, s)
                mm_acc(q + 3, vb, 16, 0, [uT[:, k, :] for k in range(16)], bAr, s)
                K.act(sg[e][:], pb[q + 1][:, 0:TP], AF.Sigmoid, [bP[q + 1]], [bSg[e]])
                K.tt("vector", m1[e][:], sg[e][:], pb[q + 0][:, 0:TP], ALU.mult, [bSg[e], bP[q + 0]], [bM1[e]])
                K.act(sg[e][:], pb[q + 3][:, 0:TP], AF.Sigmoid, [bP[q + 3]], [bSg[e]])
                K.tt("vector", m2[e][:], sg[e][:], pb[q + 2][:, 0:TP], ALU.mult, [bSg[e], bP[q + 2]], [bM2[e]])
                K.tt("gpsimd", mh[:, cb, :], m1[e][:], m2[e][:], ALU.add, [bM1[e], bM2[e]], [bMh])
            for g in range(4):
                s = wslot()
                v = load_w(s, 0, wout, 0, 16, g * 512, 512)
                for j in range(4):
                    cb = g * 4 + j
                    q = cb % 8
                    mm_acc(q, v, 16, j * 128, [mh[:, k, :] for k in range(16)], bMh, s)
                    K.copy("scalar", fb[:, cb, :], pb[q][:, 0:TP], [bP[q]], [bFb])
            rms_stats(K, D, [fb[:, b, :] for b in range(16)], bFb, TP, ones, bC, sq, bSq, pb[0], bP[0], rstd, bR)
            for b in range(16):
                t = b % 2
                K.tt("vector", tmp[t][:], fb[:, b, :], rstd[:], ALU.mult, [bFb, bR], [bT[t]])
                for lo, hi, kind in tok_ranges(p):
                    K.stt("vector", xs[:, b, lo:hi], tmp[t][:, lo:hi], c1[:, kind, b:b + 1], xs[:, b, lo:hi],
                          ALU.mult, ALU.add, [bT[t], bC, bX], [bX])
            rms_stats(K, D, [xs[:, b, :] for b in range(16)], bX, TP, ones, bC, sq, bSq, pb[0], bP[0], rstd, bR)
            for b in range(16):
                t = b % 2
                K.tt("vector", tmp[t][:], xs[:, b, :], rstd[:], ALU.mult, [bX, bR], [bT[t]])
                for lo, hi, kind in tok_ranges(p):
                    K.ts("gpsimd", mh[:, b, lo:hi], tmp[t][:, lo:hi], c2[:, kind, b:b + 1], mv[:, 3, kind, b:b + 1],
                         ALU.mult, ALU.add, [bT[t], bC], [bMh])
            aT = arena
            for g in range(16):
                s = wslot()
                v = load_w(s, 0, w1, 0, 16, g * 512, 512)
                for j in range(4):
                    fbk = g * 4 + j
                    q = fbk % 8
                    e = fbk % 2
                    mm_acc(q, v, 16, j * 128, [mh[:, k, :] for k in range(16)], bMh, s)
                    K.act(sg[e][:], pb[q][:, 0:TP], AF.Relu, [bP[q]], [bSg[e]])
                    K.tt("vector", aT[:, fbk, :], sg[e][:], sg[e][:], ALU.mult, [bSg[e]], [bAr])
            for g in range(16):
                s = wslot()
                v = load_w(s, 0, w2, 0, 64, g * 128, 128)
                cb = g
                q = cb % 8
                mm_acc(q, v, 64, 0, [aT[:, k, :] for k in range(64)], bAr, s)
                K.copy("scalar", fb[:, cb, :], pb[q][:, 0:TP], [bP[q]], [bFb])
            rms_stats(K, D, [fb[:, b, :] for b in range(16)], bFb, TP, ones, bC, sq, bSq, pb[0], bP[0], rstd, bR)
            for b in range(16):
                t = b % 2
                K.tt("vector", tmp[t][:], fb[:, b, :], rstd[:], ALU.mult, [bFb, bR], [bT[t]])
                for lo, hi, kind in tok_ranges(p):
                    K.stt("vector", xs[:, b, lo:hi], tmp[t][:, lo:hi], c3[:, kind, b:b + 1], xs[:, b, lo:hi],
                          ALU.mult, ALU.add, [bT[t], bC, bX], [bX])
            K.dma("sync", xo[:, :, tsl].rearrange("b p t -> p b t"), xs[:], reads=[bX])
        P.emit()
    return nc


from concourse.bass_utils import run_bass_kernel_spmd

BFNP = ml_dtypes.bfloat16


def make_uT(u_ctx, u_lat):
    outs = []
    for u in (u_ctx, u_lat):
        T = u.shape[0]
        up = np.zeros((T + 4, 2048), u.dtype)
        up[2:T + 2] = u
        n = T // 128
        idx = (np.arange(n)[:, None] * 128 + np.arange(132)[None, :])
        w = up[idx]
        w = w.reshape(n, 132, 16, 128).transpose(0, 3, 2, 1)
        outs.append(np.ascontiguousarray(w))
    return np.concatenate(outs, 0)


def ssd_cols(j):
    z = np.arange(512 * j, 512 * j + 512)
    x = 4096 + np.arange(512 * j, 512 * j + 512)
    B = 4096 + 4096 + np.arange(128 * j, 128 * j + 128)
    C = 4096 + 5120 + np.arange(128 * j, 128 * j + 128)
    dtf = 4096 + 6144 + np.arange(8 * j, 8 * j + 8)
    dtb = 4096 + 6144 + 64 + np.arange(8 * j, 8 * j + 8)
    return np.concatenate([z, x, B, C, dtf, dtb])


def ssd_inputs(inp, l, j, uT, NLAT, consts, cos, sin):
    cols = ssd_cols(j)
    d = {"uT": uT, "w": np.ascontiguousarray(inp["w_in"][l][:, cols])}
    cch = cols[512:512 + 768] - 4096
    cw = inp["conv_w"][l][:, cch]
    d["convw"] = np.ascontiguousarray(cw.T.reshape(6, 128, 5).transpose(1, 0, 2))
    d["convb"] = np.ascontiguousarray(inp["conv_b"][l][cch].reshape(6, 128).T)
    hv = np.stack([inp[k][l][:, 8 * j:8 * j + 8].reshape(16) for k in ("dt_bias", "a_log", "d_skip")])
    d["hv"] = np.ascontiguousarray(hv.astype(np.float32))
    d["normw"] = np.ascontiguousarray(inp["ssd_norm"][l][512 * j:512 * j + 512])
    d["cos"], d["sin"] = cos, sin
    d.update(consts)
    return d


def na_cols(j):
    cols = []
    base = 4096 + 6144 + 128
    for hd in (2 * j, 2 * j + 1):
        for part in range(3):
            cols.append(base + part * 2048 + hd * 128 + np.arange(128))
    return np.concatenate(cols)


def na_inputs(inp, l, j, uT, NLAT, identb):
    d = {"uT": uT, "wna": np.ascontiguousarray(inp["w_in"][l][:, na_cols(j)])}
    d["nabias"] = np.stack([na_bias(inp["rpb"][l][2 * j + i], NLAT) for i in range(2)])
    d["identb"] = identb
    return d


def to_fm(a):
    T, F = a.shape
    return np.ascontiguousarray(a.reshape(T, F // 128, 128).transpose(1, 2, 0))


def from_fm(a):
    nb, _, T = a.shape
    return a.transpose(2, 0, 1).reshape(T, nb * 128)


def modv_layout(mod, modc):
    m = np.stack([mod.reshape(6, 16, 128), modc.reshape(6, 16, 128)], 1)
    return np.ascontiguousarray(m.transpose(3, 0, 1, 2)).astype(np.float32)


def gains_layout(inp, l):
    g = np.stack([inp[k][l].reshape(16, 128) for k in ("g_pre_mix", "g_post_mix", "g_pre_mlp", "g_post_mlp")])
    return np.ascontiguousarray(g.transpose(2, 0, 1)).astype(np.float32)


_PROGS = {}


def _prog(name, fn):
    if name not in _PROGS:
        _PROGS[name] = fn()
    return _PROGS[name]


def kernel(**inp):
    inp = {k: np.asarray(v) for k, v in inp.items()}
    NC = 8
    cores = list(range(NC))
    NL = 2
    x = np.ascontiguousarray(inp["x"][0])
    ctx = np.ascontiguousarray(inp["ctx"][0])
    c2 = np.ascontiguousarray(np.stack([inp["c"][0], inp["c_ctx"]], -1).reshape(16, 128, 2).transpose(1, 0, 2))
    ncM = _prog("mod", build_mod)
    maps = [{"c2": c2, "wada": np.ascontiguousarray(inp["w_ada"][:, :, 1536 * j:1536 * (j + 1)]),
             "bada": np.ascontiguousarray(inp["b_ada"][:, 1536 * j:1536 * (j + 1)])} for j in cores]
    res = run_bass_kernel_spmd(ncM, maps, core_ids=cores)
    mods = np.concatenate([res.results[j]["mod"] for j in cores], axis=-1)
    consts = ssd_consts()
    cos, sin = rope_tables(8192)
    ones = np.ones((128, 128), np.float32)
    ncA = _prog("normA", build_normA)
    ncS = _prog("ssd", lambda: build_ssd(2, 64))
    ncN = _prog("na", lambda: build_na(2, 64, True))
    ncD = _prog("dense", build_dense)
    gc0 = 4096 + 6144 + 128 + 3 * 2048
    tok_idx = [np.concatenate([256 + 1024 * j + np.arange(1024), 32 * j + np.arange(32)]) for j in cores]
    for l in range(NL):
        modv = modv_layout(mods[l, 0], mods[l, 1])
        gains = gains_layout(inp, l)
        xT = [to_fm(np.concatenate([x[1024 * j:1024 * (j + 1)], ctx[32 * j:32 * (j + 1)]], 0)) for j in cores]
        maps = [{"xT": xT[j], "modv": modv, "gains": gains, "ones": ones} for j in cores]
        res = run_bass_kernel_spmd(ncA, maps, core_ids=cores)
        uTc = [res.results[j]["uTo"] for j in cores]
        u_tok = [from_fm(u) for u in uTc]
        u_lat = np.concatenate([u[:1024] for u in u_tok], 0)
        u_ctx = np.concatenate([u[1024:] for u in u_tok], 0)
        uT_all = make_uT(u_ctx, u_lat)
        maps = [ssd_inputs(inp, l, j, uT_all, 64, consts, cos, sin) for j in cores]
        res = run_bass_kernel_spmd(ncS, maps, core_ids=cores)
        g_all = np.concatenate([res.results[j]["gT"] for j in cores], 0)
        maps = [na_inputs(inp, l, j, uT_all, 64, consts["identb"]) for j in cores]
        res = run_bass_kernel_spmd(ncN, maps, core_ids=cores)
        o_all = np.concatenate([res.results[j]["oT"] for j in cores], 0)
        wgm = np.ascontiguousarray(inp["w_in"][l][:, gc0:gc0 + 4096])
        maps = []
        for j in cores:
            maps.append({"gT": np.ascontiguousarray(g_all[:, :, tok_idx[j]]),
                         "oT": np.ascontiguousarray(o_all[:, :, tok_idx[j]]),
                         "uT": uTc[j], "xT": xT[j], "modv": modv, "gains": gains, "ones": ones, "wg": wgm,
                         "wsso": inp["w_ssd_o"][l], "wnao": inp["w_na_o"][l], "wout": inp["w_out"][l],
                         "w1": inp["w_mlp1"][l], "w2": inp["w_mlp2"][l]})
        res = run_bass_kernel_spmd(ncD, maps, core_ids=cores)
        xn = [from_fm(res.results[j]["xTo"]) for j in cores]
        x = np.concatenate([t[:1024] for t in xn], 0)
        ctx = np.concatenate([t[1024:] for t in xn], 0)
    return np.ascontiguousarray(x[None].astype(np.float32))
```

```python
import contextlib
import numpy as np
import ml_dtypes
import concourse.bass as bass
import concourse.mybir as mybir
from concourse.bass_utils import run_bass_kernel_spmd

F32 = mybir.dt.float32
BF16 = mybir.dt.bfloat16
ALU = mybir.AluOpType
AF = mybir.ActivationFunctionType
AX = mybir.AxisListType

ENGS = ("tensor", "vector", "scalar", "gpsimd", "sync")


class Buf:
    __slots__ = ("name", "last_w", "readers")

    def __init__(self, name):
        self.name = name
        self.last_w = None
        self.readers = []


class Op:
    __slots__ = ("eng", "fn", "deps", "is_dma", "idx", "sem", "cum", "nodep_same", "cc")

    def __init__(self, eng, fn, is_dma, cc=False):
        self.eng = eng
        self.fn = fn
        self.deps = []
        self.is_dma = is_dma
        self.cc = cc
        self.idx = None
        self.sem = None
        self.cum = None


class Prog:
    def __init__(self, nc, n_dma_sems=6, same_engine_sync=True):
        self.nc = nc
        self.ops = []
        self.n_dma_sems = n_dma_sems
        self.same_engine_sync = same_engine_sync
        self.prologue = {}

    def add(self, eng, fn, reads=(), writes=(), dma=False, cc=False):
        op = Op(eng, fn, dma or cc, cc)
        deps = []
        for b in reads:
            if b.last_w is not None:
                deps.append(b.last_w)
        for b in writes:
            if b.last_w is not None:
                deps.append(b.last_w)
            deps.extend(b.readers)
        seen = set()
        for d in deps:
            if id(d) not in seen and d is not op:
                seen.add(id(d))
                op.deps.append(d)
        for b in reads:
            b.readers.append(op)
        for b in writes:
            b.last_w = op
            b.readers = []
        self.ops.append(op)
        return op

    def barrier(self):
        self.ops.append("BAR")

    def _skip(self, d, e):
        return d.eng == e and not d.is_dma and (not self.same_engine_sync or e == "tensor")

    def emit(self):
        nc = self.nc
        nds = self.n_dma_sems
        sig = set()
        last = {}
        for op in self.ops:
            if op == "BAR":
                for e, o in last.items():
                    sig.add(id(o))
                continue
            for d in op.deps:
                if not self._skip(d, op.eng):
                    sig.add(id(d))
            if not op.is_dma:
                last[op.eng] = op
        for e, o in last.items():
            sig.add(id(o))
        with contextlib.ExitStack() as st:
            eng_sem = {e: st.enter_context(nc.semaphore("s_" + e)) for e in ENGS}
            dma_sems = {e: [st.enter_context(nc.semaphore("d_%s%d" % (e, i))) for i in range(nds)]
                        for e in ("sync", "scalar", "gpsimd")}
            cc_sem = st.enter_context(nc.semaphore("cc_sem"))
            cnt = {e: 0 for e in ENGS}
            dcnt = {e: 0 for e in ENGS}
            dsem_cum = {}
            dsem_prev = {}
            cc_cnt = 0
            bars = []
            per_eng = {e: [] for e in ENGS}
            for op in self.ops:
                if op == "BAR":
                    snap = (dict(cnt), dict(dsem_cum), cc_cnt)
                    bars.append(snap)
                    for e in ENGS:
                        per_eng[e].append(("BAR", len(bars) - 1))
                    continue
                e = op.eng
                per_eng[e].append(op)
                if op.cc:
                    cc_cnt += 1
                    op.sem, op.cum = cc_sem, cc_cnt
                    dsem_prev[id(op)] = cc_cnt - 1
                elif op.is_dma:
                    k = dcnt[e] % nds
                    dcnt[e] += 1
                    key = (e, k)
                    prev = dsem_cum.get(key, 0)
                    op.sem, op.cum = dma_sems[e][k], prev + 16
                    dsem_prev[id(op)] = prev
                    dsem_cum[key] = prev + 16
                else:
                    if id(op) in sig:
                        cnt[e] += 1
                        op.sem, op.cum = eng_sem[e], cnt[e]
                    else:
                        op.sem, op.cum = eng_sem[e], cnt[e] + 1
            block = st.enter_context(nc.Block())

            def make(e):
                def body(eng):
                    seen = {}
                    if e in self.prologue:
                        self.prologue[e](eng)

                    def wait(s, v):
                        if v > 0 and seen.get(id(s), 0) < v:
                            eng.wait_ge(s, v)
                            seen[id(s)] = v
                    for op in per_eng[e]:
                        if isinstance(op, tuple):
                            c, dc, ccn = bars[op[1]]
                            for f in ENGS:
                                wait(eng_sem[f], c[f])
                            for (q, k), v in dc.items():
                                wait(dma_sems[q][k], v)
                            continue
                        need = {}
                        for d in op.deps:
                            if self._skip(d, e):
                                continue
                            key = id(d.sem)
                            if need.get(key, (None, 0))[1] < d.cum:
                                need[key] = (d.sem, d.cum)
                        if op.is_dma:
                            prev = dsem_prev[id(op)]
                            key = id(op.sem)
                            if prev > 0 and need.get(key, (None, 0))[1] < prev:
                                need[key] = (op.sem, prev)
                        for key, (s, v) in need.items():
                            wait(s, v)
                        ins = op.fn(eng)
                        if op.cc:
                            ins.then_inc(op.sem)
                        elif op.is_dma:
                            ins.then_inc(op.sem, 16)
                        elif id(op) in sig:
                            ins.then_inc(op.sem, 1)
                    for k in range(nds):
                        if (e, k) in dsem_cum:
                            wait(dma_sems[e][k], dsem_cum[(e, k)])
                    if e == "gpsimd":
                        wait(cc_sem, cc_cnt)
                return body

            for e in ENGS:
                if per_eng[e]:
                    getattr(block, e)(make(e))


EPS = 1e-6


class KB:
    def __init__(self, nc, P):
        self.nc = nc
        self.P = P

    def dma(self, eng, out, in_, reads=(), writes=()):
        return self.P.add(eng, lambda e: e.dma_start(out=out, in_=in_), reads, writes, dma=True)

    def mm(self, out, lhsT, rhs, start, stop, reads, writes):
        return self.P.add("tensor", lambda e: e.matmul(out, lhsT=lhsT, rhs=rhs, start=start, stop=stop), reads, writes)

    def tr(self, out, in_, ident, reads, writes):
        return self.P.add("tensor", lambda e: e.transpose(out, in_, ident), reads, writes)

    def act(self, out, in_, func, reads, writes, bias=None, scale=None, eng="scalar"):
        kw = {}
        if bias is not None:
            kw["bias"] = bias
        if scale is not None:
            kw["scale"] = scale
        return self.P.add(eng, lambda e: e.activation(out=out, in_=in_, func=func, **kw), reads, writes)

    def tt(self, eng, out, in0, in1, op, reads, writes):
        return self.P.add(eng, lambda e: e.tensor_tensor(out=out, in0=in0, in1=in1, op=op), reads, writes)

    def ts(self, eng, out, in0, s1, s2, op0, op1, reads, writes):
        if op1 is None:
            return self.P.add(eng, lambda e: e.tensor_scalar(out=out, in0=in0, scalar1=s1, scalar2=None, op0=op0), reads, writes)
        return self.P.add(eng, lambda e: e.tensor_scalar(out=out, in0=in0, scalar1=s1, scalar2=s2, op0=op0, op1=op1), reads, writes)

    def stt(self, eng, out, in0, scalar, in1, op0, op1, reads, writes):
        return self.P.add(eng, lambda e: e.scalar_tensor_tensor(out=out, in0=in0, scalar=scalar, in1=in1, op0=op0, op1=op1), reads, writes)

    def copy(self, eng, out, in_, reads, writes):
        if eng == "scalar":
            return self.P.add(eng, lambda e: e.activation(out=out, in_=in_, func=AF.Copy), reads, writes)
        return self.P.add(eng, lambda e: e.tensor_copy(out=out, in_=in_), reads, writes)

    def memset(self, eng, ap, val, writes):
        return self.P.add(eng, lambda e: e.memset(ap, val), (), writes)

    def rsum(self, eng, out, in_, reads, writes):
        return self.P.add(eng, lambda e: e.reduce_sum(out=out, in_=in_, axis=AX.X), reads, writes)


def ssd_consts():
    k = np.arange(128)
    c = {}
    c["trif"] = (k[:, None] <= k[None, :]).astype(np.float32)
    c["trib"] = (k[:, None] >= k[None, :]).astype(np.float32)
    c["maskf"] = (k[None, :] >= k[:, None]).astype(np.float32)
    c["maskb"] = (k[None, :] <= k[:, None]).astype(np.float32)
    c["identf"] = np.eye(128, dtype=np.float32)
    c["identb"] = np.eye(128, dtype=np.float32).astype(ml_dtypes.bfloat16)
    c["ones"] = np.ones((128, 128), np.float32)
    sel = np.zeros((16, 16, 128), np.float32)
    for j in range(16):
        sel[j, j, :] = 1.0
    c["sel"] = sel.reshape(16, 16 * 128)
    R = np.zeros((128, 128), np.float32)
    for base in (0, 64):
        for i in range(32):
            R[base + i, base + 32 + i] = -1.0
            R[base + 32 + i, base + i] = 1.0
    c["rt"] = np.ascontiguousarray(R.T)
    return c


def rope_tables(ntok, grid_w=64, base=10000.0):
    t = np.arange(ntok)
    row, col = t // grid_w, t % grid_w
    n_ax = 64
    inv = (base ** (-np.arange(0, n_ax, 2, dtype=np.float32) / n_ax)).astype(np.float32)
    cos = np.zeros((128, ntok), np.float32)
    sin = np.zeros((128, ntok), np.float32)
    for off, pos in ((0, row), (64, col)):
        ang = pos.astype(np.float32)[None, :] * inv[:, None]
        cs, sn = np.cos(ang).astype(np.float32), np.sin(ang).astype(np.float32)
        cos[off:off + 32] = cs
        cos[off + 32:off + 64] = cs
        sin[off:off + 32] = sn
        sin[off + 32:off + 64] = sn
    return cos, sin


NEG = -30000.0


def na_class(rp, NLAT):
    if rp == 0:
        return 0
    if rp == 1:
        return 1
    if rp == NLAT - 2:
        return 3
    if rp == NLAT - 1:
        return 4
    return 2


def na_ws(rp, NLAT):
    return int(np.clip(2 * rp - 4, 0, 2 * NLAT - 8))


def na_bias(rpb_h, NLAT):
    out = np.full((5, 128, 832), NEG, np.float32)
    out[:, :, 576:] = 0.0
    rows = 2 * NLAT
    reps = {0: 0, 1: 1, 2: 2, 3: NLAT - 2, 4: NLAT - 1}
    qc = np.arange(64)
    cstart = np.clip(qc - 8, 0, 48)
    for cls, rp in reps.items():
        ws = na_ws(rp, NLAT)
        for qr in range(2):
            r = 2 * rp + qr
            rs = int(np.clip(r - 4, 0, rows - 8))
            for i in range(9):
                kr = ws + i
                if kr < rs or kr >= rs + 8 or kr >= rows:
                    continue
                drow = kr - r + 7
                for q in range(64):
                    k0 = cstart[q]
                    kcs = np.arange(k0, k0 + 16)
                    out[cls, qr * 64 + q, i * 64 + kcs] = rpb_h[drow, kcs - q + 15]
    return out


NCORE = 8
NLAYER = 2
NCTX, NLAT = 2, 64
NCH = NCTX + NLAT
TALL = NCH * 128
TP = 352
NPASS = 3
TT = TP * NPASS
NLATC = 1024
ARENA_ELEMS = 104000


class Env:
    def __init__(self, nc, st):
        self.nc = nc
        self.P = Prog(nc)
        self.K = KB(nc, self.P)
        self.arena = st.enter_context(nc.sbuf_tensor("arena", [128, ARENA_ELEMS], BF16))
        self.off = 0
        self.pb = [st.enter_context(nc.psum_tensor("pb%d" % i, [128, 512], F32)) for i in range(8)]
        self.bP = [Buf("pb%d" % i) for i in range(8)]

    def phase(self):
        self.P.barrier()
        self.off = 0
        self.bP = [Buf("pb%d" % i) for i in range(8)]

    def sb(self, name, shape, dt):
        n = 1
        for s in shape[1:]:
            n *= s
        nbytes = n * (4 if dt == F32 else 2)
        off = self.off
        self.off += (nbytes + 3) // 4 * 4
        assert self.off <= ARENA_ELEMS * 2, ("arena overflow", name, self.off)
        v = self.arena[0:shape[0], off // 2:off // 2 + nbytes // 2]
        if dt == F32:
            v = v.bitcast(F32)
        if len(shape) == 3:
            v = v.rearrange("p (a b) -> p a b", b=shape[2])
        elif len(shape) == 4:
            v = v.rearrange("p (a b c) -> p a b c", b=shape[2], c=shape[3])
        elif len(shape) == 5:
            v = v.rearrange("p (a b c d) -> p a b c d", b=shape[2], c=shape[3], d=shape[4])
        return v

    def pbf(self, i):
        return self.pb[i][:].bitcast(BF16)


def phase_mod(env, dr):
    K, P, sb, pb, bP = env.K, env.P, env.sb, env.pb, env.bP
    cf = sb("cf", [128, 16, 2], F32)
    cb = sb("cb", [128, 16, 2], BF16)
    bad = sb("bad", [128, NLAYER, 12], F32)
    mo = sb("mo", [128, NLAYER, 12, 2], F32)
    bC, bMo = Buf("c"), Buf("mo")
    K.dma("sync", cf[:], dr["c2"], writes=[bC])
    K.dma("sync", bad[:], dr["bada"], writes=[bC])
    K.act(cb[:], cf[:], AF.Silu, [bC], [bC])
    Wt = [sb("W%d" % i, [128, 16, 512], BF16) for i in range(2)]
    bW = [Buf("W0"), Buf("W1")]
    i = 0
    for l in range(NLAYER):
        for ct in range(3):
            s = i % 2
            i += 1
            K.dma("gpsimd", Wt[s][:], dr["wada"][l][:, ct * 512:(ct + 1) * 512].rearrange("(kc p) n -> p kc n", p=128),
                  writes=[bW[s]])
            for j in range(4):
                b12 = ct * 4 + j
                q = b12 % 4
                for kc in range(16):
                    K.mm(pb[q][:, 0:2], Wt[s][:, kc, j * 128:(j + 1) * 128], cb[:, kc, :], kc == 0, kc == 15,
                         [bC, bW[s]], [bP[q]])
                K.ts("vector", mo[:, l, b12, :], pb[q][:, 0:2], bad[:, l, b12:b12 + 1], None, ALU.add, None,
                     [bP[q], bC], [bMo])
    K.dma("sync", dr["ag_m_in"].rearrange("(l p) x -> p l x", p=128), mo[:].rearrange("p l b k -> p l (b k)"),
          reads=[bMo], writes=[dr["b_ag_m_in"]])
    P.add("gpsimd", lambda e: e.collective_compute("AllGather", ALU.bypass, replica_groups=[list(range(NCORE))],
                                                   ins=[dr["ag_m_in"].opt()], outs=[dr["ag_m"].opt()]),
          reads=[dr["b_ag_m_in"]], writes=[dr["b_ag_m"]], cc=True)


def load_mods(env, dr, l, bC):
    K, sb = env.K, env.sb
    mvs = sb("mvs", [128, 8, 6, 2, 2], F32)
    gn = sb("gn", [128, 4, 16], F32)
    ones = sb("ones_s", [128, 128], F32)
    K.dma("sync", mvs[:].rearrange("p r i b k -> p r (i b k)"),
          dr["ag_m"].rearrange("(r l p) x -> l p r x", r=NCORE, p=128)[l], reads=[dr["b_ag_m"]], writes=[bC])
    K.dma("sync", gn[:], dr["gains"][l], writes=[bC])
    K.dma("sync", ones[:], dr["ones"], writes=[bC])
    return mvs, gn, ones


def mv_all(mvs, i, kind):
    return mvs[:, :, i, :, kind]


def c16(t, kind):
    return t[:, kind, :].rearrange("p (r b) -> p r b", b=2)


def gn16(gn, i):
    return gn[:, i, :].rearrange("p (r b) -> p r b", b=2)


def rms_stats(K, src_blocks, bsrc, T, ones, bC, sq, bSq, pst, bPst, rstd_bc, bR, nfeat=2048.0):
    nb = len(src_blocks)
    for b, ap in enumerate(src_blocks):
        s = b % 2
        K.act(sq[s][:, 0:T], ap, AF.Square, [bsrc], [bSq[s]])
        K.mm(pst[:, 0:T], ones[:], sq[s][:, 0:T], b == 0, b == nb - 1, [bC, bSq[s]], [bPst])
    K.act(rstd_bc[:, 0:T], pst[:, 0:T], AF.Sqrt, [bPst], [bR], bias=EPS, scale=1.0 / nfeat)
    K.P.add("vector", lambda e: e.reciprocal(out=rstd_bc[:, 0:T], in_=rstd_bc[:, 0:T]), [bR], [bR])


def tok_ranges(p):
    g0, g1 = p * TP, (p + 1) * TP
    out = []
    if g0 < NLATC:
        out.append((0, min(g1, NLATC) - g0, 0))
    if g1 > NLATC:
        out.append((max(g0, NLATC) - g0, TP, 1))
    return out


def phase_normA(env, dr, l, xsrc, bxsrc):
    K, P, sb, pb, bP = env.K, env.P, env.sb, env.pb, env.bP
    bC = Buf("c")
    mvs, gn, ones = load_mods(env, dr, l, bC)
    cA = sb("cA", [128, 2, 16], F32)
    sh = sb("sh", [128, 2, 16], F32)
    for kind in range(2):
        K.ts("vector", c16(cA, kind), mv_all(mvs, 1, kind), 1.0, None, ALU.add, None, [bC], [bC])
        K.tt("vector", c16(cA, kind), c16(cA, kind), gn16(gn, 0), ALU.mult, [bC], [bC])
        K.copy("vector", c16(sh, kind), mv_all(mvs, 0, kind), [bC], [bC])
    xs = [sb("xs%d" % i, [128, 16, TP], F32) for i in range(2)]
    bX = [Buf("x0"), Buf("x1")]
    us = [sb("us%d" % i, [128, 16, TP], BF16) for i in range(2)]
    bU = [Buf("u0"), Buf("u1")]
    sq = [sb("sq%d" % i, [128, TP], F32) for i in range(2)]
    bSq = [Buf("sq0"), Buf("sq1")]
    rstd = sb("rstd", [128, TP], F32)
    bR = Buf("rstd")
    tmp = [sb("tmp%d" % i, [128, TP], F32) for i in range(2)]
    bT = [Buf("t0"), Buf("t1")]
    ag_in = dr["ag_u_in"].rearrange("(b p) t -> b p t", p=128)
    for p in range(NPASS):
        s = p % 2
        K.dma("sync", xs[s][:], xsrc[:, :, p * TP:(p + 1) * TP].rearrange("b p t -> p b t"), reads=[bxsrc], writes=[bX[s]])
        rms_stats(K, [xs[s][:, b, :] for b in range(16)], bX[s], TP, ones, bC, sq, bSq, pb[0], bP[0], rstd, bR)
        for b in range(16):
            t = b % 2
            K.tt("vector", tmp[t][:], xs[s][:, b, :], rstd[:], ALU.mult, [bX[s], bR], [bT[t]])
            for lo, hi, kind in tok_ranges(p):
                K.ts("gpsimd", us[s][:, b, lo:hi], tmp[t][:, lo:hi], cA[:, kind, b:b + 1], sh[:, kind, b:b + 1],
                     ALU.mult, ALU.add, [bT[t], bC], [bU[s]])
        K.dma("sync", ag_in[:, :, p * TP:(p + 1) * TP].rearrange("b p t -> p b t"), us[s][:], reads=[bU[s]],
              writes=[dr["b_ag_u_in"]])
    P.add("gpsimd", lambda e: e.collective_compute("AllGather", ALU.bypass, replica_groups=[list(range(NCORE))],
                                                   ins=[dr["ag_u_in"].opt()], outs=[dr["ag_u"].opt()]),
          reads=[dr["b_ag_u_in"]], writes=[dr["b_ag_u"]], cc=True)


def load_ut(env, dr, c, dst, bdst3):
    K = env.K
    agu = dr["ag_u"].rearrange("(r k p) t -> r p k t", r=NCORE, p=128)
    rd = [dr["b_ag_u"]]
    bmain, bleft, bright = bdst3
    if c >= NCTX:
        lc = c - NCTX
        j, o = lc // 8, (lc % 8) * 128
        lo, hi = max(o - 2, 0), min(o + 130, NLATC)
        K.dma("sync", dst[:, :, 2 - (o - lo):2 - (o - lo) + (hi - lo)], agu[j][:, :, lo:hi], reads=rd,
              writes=[bmain] + ([bleft] if o > 0 else []) + ([bright] if o + 128 < NLATC else []))
        if o == 0:
            if lc == 0:
                K.memset("gpsimd", dst[:, :, 0:2], 0.0, [bleft])
            else:
                K.dma("sync", dst[:, :, 0:2], agu[j - 1][:, :, NLATC - 2:NLATC], reads=rd, writes=[bleft])
        if o + 128 == NLATC:
            if lc == NLAT - 1:
                K.memset("gpsimd", dst[:, :, 130:132], 0.0, [bright])
            else:
                K.dma("sync", dst[:, :, 130:132], agu[j + 1][:, :, 0:2], reads=rd, writes=[bright])
    else:
        for q in range(4):
            K.dma("sync", dst[:, :, 2 + 32 * q:2 + 32 * (q + 1)], agu[4 * c + q][:, :, NLATC:NLATC + 32], reads=rd,
                  writes=[bmain])
        if c == 0:
            K.memset("gpsimd", dst[:, :, 0:2], 0.0, [bleft])
        else:
            K.dma("sync", dst[:, :, 0:2], agu[4 * c - 1][:, :, NLATC + 30:NLATC + 32], reads=rd, writes=[bleft])
        if c == NCTX - 1:
            K.memset("gpsimd", dst[:, :, 130:132], 0.0, [bright])
        else:
            K.dma("sync", dst[:, :, 130:132], agu[4 * c + 4][:, :, NLATC:NLATC + 2], reads=rd, writes=[bright])

def phase_ssd(env, dr, l):
    K, P, sb, bP = env.K, env.P, env.sb, env.bP
    nc = env.nc
    pb = env.pb[0:7]
    pbf = env.pbf(7)
    bPbf = env.bP[7]
    w = dr["w_ssd"][l]
    convw, convb, hv, normw = dr["convw"][l], dr["convb"][l], dr["hv"][l], dr["normw"][l]
    cosd, sind = dr["cos"], dr["sin"]
    cn = dr["cn"]
    sc_xs, sc_B, sc_CT, sc_cbb, sc_yf = dr["sc_xs"], dr["sc_B"], dr["sc_CT"], dr["sc_cbb"], dr["sc_yf"]
    gT = dr["ag_g_in"].rearrange("(b p) t -> b p t", p=128)
    Wb = sb("Wb", [128, 16, 1296], BF16)
    bW = Buf("Wb")
    K.dma("gpsimd", Wb[:], w.rearrange("(kc p) n -> p kc n", p=128), writes=[bW])
    csb = {}
    bC = Buf("consts")
    for n in cn:
        shp = [16, 2048] if n == "sel" else [128, 128]
        csb[n] = sb("c_" + n, shp, BF16 if n == "identb" else F32)
        K.dma("sync", csb[n][:], cn[n], writes=[bC])
    cw = sb("cw", [128, 6, 5], F32)
    cbias = sb("cbias", [128, 6], F32)
    K.dma("sync", cw[:], convw, writes=[bC])
    K.dma("sync", cbias[:], convb, writes=[bC])
    hvb = sb("hvb", [128, 3, 16], F32)
    K.dma("sync", hvb[:], hv.partition_broadcast(128), writes=[bC])
    nwb = sb("nwb", [128, 512], F32)
    K.dma("sync", nwb[:], normw.partition_broadcast(128), writes=[bC])
    a_bc = sb("a_bc", [128, 16], F32)
    dsum = sb("dsum", [128, 8], F32)
    K.act(a_bc[:], hvb[:, 1, :], AF.Exp, [bC], [bC])
    K.ts("vector", a_bc[:], a_bc[:], -1.0, None, ALU.mult, None, [bC], [bC])
    K.tt("vector", dsum[:], hvb[:, 2, 0:8], hvb[:, 2, 8:16], ALU.add, [bC], [bC])

    ut = [sb("ut%d" % i, [128, 16, 132], BF16) for i in range(2)]
    bUt = [[Buf("ut%d%s" % (i, x)) for x in "mlr"] for i in range(2)]
    NJ = NCH * 16
    dtraw = sb("dtraw", [128, NCH, 16], F32)
    dtv = sb("dtv", [128, NCH, 16], F32)
    da = sb("da", [128, NCH, 16], F32)
    cs = sb("cs", [128, NCH, 16], F32)
    tot = sb("tot", [128, NCH, 16], F32)
    dchunk = sb("dchunk", [128, NCH, 16], F32)
    dece = sb("dece", [128, NCH, 16], F32)
    ecs = sb("ecs", [128, NCH, 16], F32)
    negcs = sb("negcs", [128, NCH, 16], F32)
    csT = sb("csT", [16, NCH * 128], F32)
    bD = Buf("dstuff")
    for c0 in range(0, NCH, 32):
        n = min(32, NCH - c0)
        for ci in range(n):
            c = c0 + ci
            s = c % 2
            load_ut(env, dr, c, ut[s], bUt[s])
            for kc in range(16):
                K.mm(pb[0][:, ci * 16:(ci + 1) * 16], ut[s][:, kc, 2:130], Wb[:, kc, 1280:1296],
                     kc == 0, kc == 15, [*bUt[s], bW], [bP[0]])
        K.copy("scalar", dtraw[:, c0:c0 + n, :], pb[0][:, 0:n * 16].rearrange("p (c j) -> p c j", j=16),
               [bP[0]], [bD])
    K.tt("vector", dtv[:], dtraw[:], hvb[:, 0:1, :].to_broadcast([128, NCH, 16]), ALU.add, [bD, bC], [bD])
    K.act(dtv[:], dtv[:], AF.Exp, [bD], [bD])
    K.act(dtv[:], dtv[:], AF.Ln, [bD], [bD], bias=1.0)
    K.tt("vector", da[:], dtv[:], a_bc[:].unsqueeze(1).to_broadcast([128, NCH, 16]), ALU.mult, [bD, bC], [bD])
    for c0 in range(0, NCH, 32):
        n = min(32, NCH - c0)
        for d, tri in ((0, "trif"), (1, "trib")):
            K.mm(pb[1][:, 0:n * 8].rearrange("p (c j) -> p c j", j=8), csb[tri][:], da[:, c0:c0 + n, d * 8:d * 8 + 8],
                 True, True, [bD, bC], [bP[1]])
            K.copy("scalar", cs[:, c0:c0 + n, d * 8:d * 8 + 8], pb[1][:, 0:n * 8].rearrange("p (c j) -> p c j", j=8),
                   [bP[1]], [bD])
        K.mm(pb[2][:, 0:n * 16].rearrange("p (c j) -> p c j", j=16), csb["ones"][:], da[:, c0:c0 + n, :],
             True, True, [bD, bC], [bP[2]])
        K.copy("scalar", tot[:, c0:c0 + n, :], pb[2][:, 0:n * 16].rearrange("p (c j) -> p c j", j=16), [bP[2]], [bD])
    K.act(dchunk[:], tot[:], AF.Exp, [bD], [bD])
    K.tt("vector", dece[:], tot[:], cs[:], ALU.subtract, [bD], [bD])
    K.act(dece[:], dece[:], AF.Exp, [bD], [bD])
    K.act(ecs[:], cs[:], AF.Exp, [bD], [bD])
    K.ts("vector", negcs[:], cs[:], -1.0, None, ALU.mult, None, [bD], [bD])
    for c0 in range(0, NCH, 4):
        n = min(4, NCH - c0)
        for ci in range(n):
            K.tr(pb[3][0:16, ci * 128:(ci + 1) * 128], cs[:, c0 + ci, :], csb["identf"][:], [bD, bC], [bP[3]])
        K.copy("scalar", csT[:, c0 * 128:(c0 + n) * 128], pb[3][0:16, 0:n * 128], [bP[3]], [bD])

    acc = [sb("acc%d" % i, [128, 128], F32) for i in range(6)]
    bAcc = [Buf("acc%d" % i) for i in range(6)]
    xsT = [sb("xsT%d" % i, [128, 128], F32) for i in range(6)]
    bXsT = [Buf("xsT%d" % i) for i in range(6)]
    cos_t = sb("cos_t", [128, 128], F32)
    sin_t = sb("sin_t", [128, 128], F32)
    bCos = Buf("cos")
    rt1 = sb("rt1", [128, 2, 128], F32)
    rt2 = sb("rt2", [128, 2, 128], F32)
    bRt1, bRt2 = Buf("rt1"), Buf("rt2")
    BT = sb("BT", [128, 128], BF16)
    CT = [sb("CT%d" % i, [128, 128], BF16) for i in range(2)]
    bBT = Buf("BT")
    bCT = [Buf("CT0"), Buf("CT1")]
    xs_tm = [sb("xs_tm%d" % i, [128, 512], F32) for i in range(2)]
    bXs = [Buf("xs_tm0"), Buf("xs_tm1")]
    B_tm = [sb("B_tm%d" % i, [128, 128], BF16) for i in range(2)]
    bBtm = [Buf("B_tm0"), Buf("B_tm1")]
    cbm = [sb("cbm%d" % i, [128, 128], F32) for i in range(2)]
    bCbm = [Buf("cbm0"), Buf("cbm1")]
    cbb = [sb("cbb%d" % i, [128, 128], F32) for i in range(2)]
    bCbb = [Buf("cbb0"), Buf("cbb1")]
    xdt = sb("xdt", [128, 512], BF16)
    xw = sb("xw", [128, 512], BF16)
    bXdt, bXw = Buf("xdt"), Buf("xw")
    Esb = [sb("E%d" % i, [128, 128], F32) for i in range(8)]
    bE = [Buf("E%d" % i) for i in range(8)]
    MT = [sb("MT%d" % i, [128, 128], BF16) for i in range(8)]
    bMT = [Buf("MT%d" % i) for i in range(8)]
    ytmp = sb("ytmp", [128, 512], F32)
    bYtmp = Buf("ytmp")
    yy = [sb("yy%d" % i, [128, 512], F32) for i in range(2)]
    bYy = [Buf("yy0"), Buf("yy1")]
    yfl = [sb("yfl%d" % i, [128, 512], F32) for i in range(2)]
    bYfl = [Buf("yfl0"), Buf("yfl1")]
    xsd = sb("xsd", [128, 512], F32)
    bXsd = Buf("xsd")
    hst = [sb("hst%d" % i, [128, 512], F32) for i in range(2)]
    hbf = [sb("hbf%d" % i, [128, 512], BF16) for i in range(2)]
    bH = [Buf("h0"), Buf("h1")]
    bHbf = [Buf("hbf0"), Buf("hbf1")]
    htmp = sb("htmp", [128, 512], F32)
    bHtmp = Buf("htmp")
    sz = sb("sz", [128, 512], F32)
    gg = sb("gg", [128, 512], F32)
    gsq = sb("gsq", [128, 512], F32)
    ssum = sb("ssum", [128, 1], F32)
    rstd = sb("rstd", [128, 1], F32)
    gn = sb("gn", [128, 512], BF16)
    gTs = [sb("gTs%d" % i, [128, 4, 128], BF16) for i in range(2)]
    bSz, bGg, bGsq, bSs, bRstd, bGn = Buf("sz"), Buf("gg"), Buf("gsq"), Buf("ss"), Buf("rstd"), Buf("gn")
    bGTs = [Buf("gTs0"), Buf("gTs1")]
    for d in range(2):
        K.memset("vector", hst[d][:], 0.0, [bH[d]])
        K.memset("vector", hbf[d][:], 0.0, [bHbf[d]])

    def bc8(t, c, j0):
        return t[:, c, j0:j0 + 8].unsqueeze(2).to_broadcast([128, 8, 64])

    def v3(ap):
        return ap.rearrange("p (h d) -> p h d", h=8)

    def scan_step(c, d, xs_ap, bxs, Btm_ap, bbtm, CT_ap, bct, cbm_ap, bcbm, y_out, by_out):
        j0 = d * 8
        K.tt("gpsimd", v3(xdt[:]), v3(xs_ap), bc8(dtv, c, j0), ALU.mult, [bxs, bD], [bXdt])
        K.tt("gpsimd", v3(xw[:]), v3(xdt[:]), bc8(dece, c, j0), ALU.mult, [bXdt, bD], [bXw])
        regs = [pb[3 + (h // 4)][:, (h % 4) * 128:(h % 4 + 1) * 128] for h in range(8)]
        for h in range(8):
            j = j0 + h
            K.mm(regs[h], csb["sel"][:, j * 128:(j + 1) * 128], csT[:, c * 128:(c + 1) * 128], True, True,
                 [bC, bD], [bP[3 + (h // 4)]])
        for h in range(8):
            j = j0 + h
            K.act(Esb[h][:], regs[h], AF.Exp, [bP[3 + (h // 4)], bD], [bE[h]], bias=negcs[:, c, j:j + 1])
        for h in range(8):
            K.stt("vector", MT[h][:], Esb[h][:], 1.0, cbm_ap, ALU.min, ALU.mult, [bE[h], bcbm], [bMT[h]])
        for h in range(8):
            K.mm(pb[5][:, h * 64:(h + 1) * 64], MT[h][:], xdt[:, h * 64:(h + 1) * 64], True, True,
                 [bMT[h], bXdt], [bP[5]])
        K.mm(pb[6][:], CT_ap, hbf[d][:], True, True, [bct, bHbf[d]], [bP[6]])
        K.mm(pb[2][:], Btm_ap, xw[:], True, True, [bbtm, bXw], [bP[2]])
        K.tt("vector", v3(ytmp[:]), v3(pb[6][:]), bc8(ecs, c, j0), ALU.mult, [bP[6], bD], [bYtmp])
        K.tt("vector", y_out, ytmp[:], pb[5][:], ALU.add, [bYtmp, bP[5]], [by_out])
        K.tt("gpsimd", v3(htmp[:]), v3(hst[d][:]), bc8(dchunk, c, j0), ALU.mult, [bH[d], bD], [bHtmp])
        K.tt("vector", hst[d][:], htmp[:], pb[2][:], ALU.add, [bHtmp, bP[2]], [bH[d]])
        K.copy("scalar", hbf[d][:], hst[d][:], [bH[d]], [bHbf[d]])

    bSc = dr['bSc']
    def prepF(c):
        s = c % 2
        lat = c >= NCTX
        load_ut(env, dr, c, ut[s], bUt[s])
        if lat:
            lc = c - NCTX
            K.dma("sync", cos_t[:], cosd[:, lc * 128:(lc + 1) * 128], writes=[bCos])
            K.dma("sync", sin_t[:], sind[:, lc * 128:(lc + 1) * 128], writes=[bCos])
        for blk in range(6):
            reg = pb[blk // 3][:, (blk % 3) * 132:(blk % 3 + 1) * 132]
            for kc in range(16):
                K.mm(reg, Wb[:, kc, 512 + blk * 128:512 + (blk + 1) * 128], ut[s][:, kc, :], kc == 0, kc == 15,
                     [bW, *bUt[s]], [bP[blk // 3]])
        for j in range(5):
            for blk in range(6):
                reg = pb[blk // 3][:, (blk % 3) * 132 + j:(blk % 3) * 132 + j + 128]
                if j == 0:
                    K.ts("vector", acc[blk][:], reg, cw[:, blk, 0:1], None, ALU.mult, None,
                         [bP[blk // 3], bC], [bAcc[blk]])
                else:
                    K.stt("vector", acc[blk][:], reg, cw[:, blk, j:j + 1], acc[blk][:], ALU.mult, ALU.add,
                          [bP[blk // 3], bC, bAcc[blk]], [bAcc[blk]])
        for blk in range(6):
            K.act(xsT[blk][:], acc[blk][:], AF.Silu, [bAcc[blk], bC], [bXsT[blk]], bias=cbias[:, blk:blk + 1])
        if lat:
            K.mm(pb[1][:, 0:128], csb["rt"][:], xsT[4][:], True, True, [bC, bXsT[4]], [bP[1]])
            K.mm(pb[1][:, 128:256], csb["rt"][:], xsT[5][:], True, True, [bC, bXsT[5]], [bP[1]])
            for i, blk in enumerate((4, 5)):
                K.tt("gpsimd", rt1[:, i, :], xsT[blk][:], cos_t[:], ALU.mult, [bXsT[blk], bCos], [bRt1])
            K.tt("vector", rt2[:], pb[1][:, 0:256].rearrange("p (i n) -> p i n", i=2),
                 sin_t[:].unsqueeze(1).to_broadcast([128, 2, 128]), ALU.mult, [bP[1], bCos], [bRt2])
            K.tt("gpsimd", BT[:], rt1[:, 0, :], rt2[:, 0, :], ALU.add, [bRt1, bRt2], [bBT])
            K.tt("gpsimd", CT[s][:], rt1[:, 1, :], rt2[:, 1, :], ALU.add, [bRt1, bRt2], [bCT[s]])
        else:
            K.copy("scalar", BT[:], xsT[4][:], [bXsT[4]], [bBT])
            K.copy("scalar", CT[s][:], xsT[5][:], [bXsT[5]], [bCT[s]])
        for blk in range(4):
            K.tr(pb[0][:, blk * 128:(blk + 1) * 128], xsT[blk][:], csb["identf"][:], [bXsT[blk], bC], [bP[0]])
        K.copy("scalar", xs_tm[s][:], pb[0][:], [bP[0]], [bXs[s]])
        K.tr(pbf[:, 0:128], BT[:], csb["identb"][:], [bBT, bC], [bPbf])
        K.copy("scalar", B_tm[s][:], pbf[:, 0:128], [bPbf], [bBtm[s]])
        K.mm(pb[1][:, 256:384], BT[:], CT[s][:], True, True, [bBT, bCT[s]], [bP[1]])
        K.tt("vector", cbm[s][:], pb[1][:, 256:384], csb["maskf"][:], ALU.mult, [bP[1], bC], [bCbm[s]])
        K.tt("vector", cbb[s][:], pb[1][:, 256:384], csb["maskb"][:], ALU.mult, [bP[1], bC], [bCbb[s]])
        K.dma("sync", sc_xs[c], xs_tm[s][:], reads=[bXs[s]], writes=[bSc['xs'][c]])
        K.dma("sync", sc_B[c], B_tm[s][:], reads=[bBtm[s]], writes=[bSc['B'][c]])
        K.dma("sync", sc_CT[c], CT[s][:], reads=[bCT[s]], writes=[bSc['CT'][c]])
        K.dma("sync", sc_cbb[c], cbb[s][:], reads=[bCbb[s]], writes=[bSc['cbb'][c]])

    def scanF(c):
        s = c % 2
        scan_step(c, 0, xs_tm[s][:], bXs[s], B_tm[s][:], bBtm[s], CT[s][:], bCT[s], cbm[s][:], bCbm[s],
                  yy[s][:], bYy[s])
        K.tt("gpsimd", v3(xsd[:]), v3(xs_tm[s][:]), dsum[:].unsqueeze(2).to_broadcast([128, 8, 64]), ALU.mult,
             [bXs[s], bC], [bXsd])
        K.tt("gpsimd", yy[s][:], yy[s][:], xsd[:], ALU.add, [bYy[s], bXsd], [bYy[s]])
        K.dma("sync", sc_yf[c], yy[s][:], reads=[bYy[s]], writes=[bSc['yf'][c]])


    prepF(0)
    for c in range(NCH):
        if c + 1 < NCH:
            prepF(c + 1)
        scanF(c)

    order = list(range(NCTX - 1, -1, -1)) + list(range(NCH - 1, NCTX - 1, -1))
    for i, c in enumerate(order):
        s = i % 2
        load_ut(env, dr, c, ut[s], bUt[s])
        K.dma("sync", xs_tm[s][:], sc_xs[c], reads=[bSc['xs'][c]], writes=[bXs[s]])
        K.dma("sync", B_tm[s][:], sc_B[c], reads=[bSc['B'][c]], writes=[bBtm[s]])
        K.dma("sync", CT[s][:], sc_CT[c], reads=[bSc['CT'][c]], writes=[bCT[s]])
        K.dma("sync", cbb[s][:], sc_cbb[c], reads=[bSc['cbb'][c]], writes=[bCbb[s]])
        K.dma("sync", yfl[s][:], sc_yf[c], reads=[bSc['yf'][c]], writes=[bYfl[s]])
        for kc in range(16):
            K.mm(pb[0][:], ut[s][:, kc, 2:130], Wb[:, kc, 0:512], kc == 0, kc == 15, [*bUt[s], bW], [bP[0]])
        K.act(sz[:], pb[0][:], AF.Silu, [bP[0]], [bSz])
        scan_step(c, 1, xs_tm[s][:], bXs[s], B_tm[s][:], bBtm[s], CT[s][:], bCT[s], cbb[s][:], bCbb[s],
                  yy[s][:], bYy[s])
        K.tt("gpsimd", yy[s][:], yy[s][:], yfl[s][:], ALU.add, [bYy[s], bYfl[s]], [bYy[s]])
        K.tt("vector", gg[:], yy[s][:], sz[:], ALU.mult, [bYy[s], bSz], [bGg])
        K.act(gsq[:], gg[:], AF.Square, [bGg], [bGsq])
        K.rsum("vector", ssum[:], gsq[:], [bGsq], [bSs])
        K.act(rstd[:], ssum[:], AF.Sqrt, [bSs], [bRstd], bias=EPS, scale=1.0 / 512.0)
        K.P.add("vector", lambda e: e.reciprocal(out=rstd[:], in_=rstd[:]), [bRstd], [bRstd])
        K.stt("vector", gn[:], gg[:], rstd[:, 0:1], nwb[:], ALU.mult, ALU.mult, [bGg, bRstd, bC], [bGn])
        for blk in range(4):
            K.tr(pbf[:, 128 + blk * 128:128 + (blk + 1) * 128], gn[:, blk * 128:(blk + 1) * 128], csb["identb"][:],
                 [bGn, bC], [bPbf])
        K.copy("scalar", gTs[s][:], pbf[:, 128:640].rearrange("p (b t) -> p b t", b=4), [bPbf], [bGTs[s]])
        K.dma("sync", gT[:, :, c * 128:(c + 1) * 128].rearrange("b p t -> p b t"), gTs[s][:], reads=[bGTs[s]], writes=[dr["b_ag_g_in"][c]])

    P.add("gpsimd", lambda e: e.collective_compute("AllGather", ALU.bypass, replica_groups=[list(range(NCORE))],
                                                   ins=[dr["ag_g_in"].opt()], outs=[dr["ag_g"].opt()]),
          reads=dr["b_ag_g_in"], writes=[dr["b_ag_g"]], cc=True)

def phase_na(env, dr, l, ctx_out):
    K, P, sb, bP = env.K, env.P, env.sb, env.bP
    w = dr["w_na"][l]
    biasd = dr["nabias"][l]
    identb_d = dr["cn"]["identb"]
    oT = dr["ag_o_in"].rearrange("(h p) t -> h p t", p=128)
    pq = env.pb[0:4]
    bPq = env.bP[0:4]
    pA, pB, pO = env.pb[4], env.pb[5], env.pb[6]
    pT = env.pbf(7)
    bPA, bPB, bPO, bPT = env.bP[4], env.bP[5], env.bP[6], env.bP[7]
    Wb = sb("Wb_s", [128, 16, 768], BF16)
    bW = Buf("Wb")
    K.dma("gpsimd", Wb[:], w.rearrange("(kc p) n -> p kc n", p=128), writes=[bW])
    identb = sb("identb_s", [128, 128], BF16)
    bC = Buf("c")
    K.dma("sync", identb[:], identb_d, writes=[bC])
    ut = [sb("ut%d" % i, [128, 16, 132], BF16) for i in range(2)]
    bUt = [[Buf("ut%d%s" % (i, x)) for x in "mlr"] for i in range(2)]
    QT = sb("QT", [128, NCH * 128], BF16)
    KT = sb("KT", [128, NCH * 128 + 64], BF16)
    Vt = sb("Vt", [128, NCH + 1, 128], BF16)
    bQ, bK, bV = Buf("QT"), Buf("KT"), Buf("Vt")
    bias = sb("bias", [128, 5, 832], F32)
    bB = Buf("bias")
    S_sb = [sb("S%d" % i, [128, 832], F32) for i in range(2)]
    bS = [Buf("S0"), Buf("S1")]
    Pf = sb("Pf", [128, 832], F32)
    bPf = Buf("Pf")
    Pn = sb("Pn", [128, 832], BF16)
    bPn = Buf("Pn")
    PT = sb("PT", [128, 7, 128], BF16)
    bPTs = Buf("PTs")
    mx = sb("mx", [128, 1], F32)
    nmx = sb("nmx", [128, 1], F32)
    rs = sb("rs", [128, 1], F32)
    rinv = sb("rinv", [128, 1], F32)
    bMx, bNmx, bRs, bRinv = Buf("mx"), Buf("nmx"), Buf("rs"), Buf("rinv")
    oTs = [sb("oTs%d" % i, [128, 128], BF16) for i in range(2)]
    bOT = [Buf("oTs0"), Buf("oTs1")]
    SCALE = 128.0 ** -0.5
    K.memset("vector", KT[:, NCH * 128:NCH * 128 + 64], 0.0, [bK])
    K.memset("vector", Vt[:, NCH, :], 0.0, [bV])

    def softmax_pv(nk, nchunks_k, kparts, vparts, q_chunk, out_dram, slot):
        K.P.add("vector", lambda e: e.reduce_max(out=mx[:], in_=S_sb[slot][:, 0:nk], axis=AX.X), [bS[slot]], [bMx])
        K.ts("vector", nmx[:], mx[:], -1.0, None, ALU.mult, None, [bMx], [bNmx])
        K.act(Pf[:, 0:nk], S_sb[slot][:, 0:nk], AF.Exp, [bS[slot], bNmx], [bPf], bias=nmx[:, 0:1])
        K.rsum("vector", rs[:], Pf[:, 0:nk], [bPf], [bRs])
        K.P.add("vector", lambda e: e.reciprocal(out=rinv[:], in_=rs[:]), [bRs], [bRinv])
        K.ts("vector", Pn[:, 0:nk], Pf[:, 0:nk], rinv[:, 0:1], None, ALU.mult, None, [bPf, bRinv], [bPn])
        for i, (c0, ncol) in enumerate(kparts):
            K.tr(pT[0:ncol, i * 128:(i + 1) * 128], Pn[:, c0:c0 + ncol], identb[:], [bPn, bC], [bPT])
        n = len(kparts)
        K.copy("scalar", PT[:, 0:n, :], pT[:, 0:n * 128].rearrange("p (c q) -> p c q", q=128), [bPT], [bPTs])
        for i, (vap, kk) in enumerate(vparts):
            K.mm(pO[:, 0:128], vap, PT[0:kk, i, :], i == 0, i == n - 1, [bV, bPTs], [bPO])
        K.copy("scalar", oTs[slot][:], pO[:, 0:128], [bPO], [bOT[slot]])
        K.dma("sync", out_dram, oTs[slot][:], reads=[bOT[slot]], writes=[dr["b_ag_o_in"][q_chunk]])

    for hd in range(2):
        K.dma("sync", bias[:], biasd[hd].rearrange("c p k -> p c k"), writes=[bB])
        for c in range(NCH):
            s = c % 2
            load_ut(env, dr, c, ut[s], bUt[s])
            for i, (dst, bdst) in enumerate(((QT, bQ), (KT, bK))):
                pp = pq[i]
                for kc in range(16):
                    K.mm(pp[:, 0:128], Wb[:, kc, hd * 384 + i * 128:hd * 384 + (i + 1) * 128], ut[s][:, kc, 2:130],
                         kc == 0, kc == 15, [bW, *bUt[s]], [bPq[i]])
                if i == 0:
                    K.act(dst[:, c * 128:(c + 1) * 128], pp[:, 0:128], AF.Copy, [bPq[i]], [bdst], scale=SCALE)
                else:
                    K.copy("vector", dst[:, c * 128:(c + 1) * 128], pp[:, 0:128], [bPq[i]], [bdst])
            for kc in range(16):
                K.mm(pq[2][:, 0:128], ut[s][:, kc, 2:130], Wb[:, kc, hd * 384 + 256:hd * 384 + 384],
                     kc == 0, kc == 15, [bW, *bUt[s]], [bPq[2]])
            K.copy("scalar", Vt[:, c, :], pq[2][:, 0:128], [bPq[2]], [bV])
        KTc = KT[:, 0:NCTX * 128]
        for rp in range(NLAT):
            slot = rp % 2
            ws = na_ws(rp, NLAT)
            cls = na_class(rp, NLAT)
            qc = NCTX + rp
            k0 = NCTX * 128 + 64 * ws
            qap = QT[:, qc * 128:(qc + 1) * 128]
            K.mm(pA[:], qap, KT[:, k0:k0 + 512], True, True, [bQ, bK], [bPA])
            K.mm(pB[:, 0:64], qap, KT[:, k0 + 512:k0 + 576], True, True, [bQ, bK], [bPB])
            K.mm(pB[:, 64:64 + NCTX * 128], qap, KTc, True, True, [bQ, bK], [bPB])
            K.tt("vector", S_sb[slot][:, 0:512], pA[:], bias[:, cls, 0:512], ALU.add, [bPA, bB], [bS[slot]])
            K.tt("vector", S_sb[slot][:, 512:832], pB[:, 0:320], bias[:, cls, 512:832], ALU.add, [bPB, bB], [bS[slot]])
            kparts = [(i * 128, 128) for i in range(4)] + [(512, 64)] + [(576 + i * 128, 128) for i in range(NCTX)]
            vc0 = NCTX + ws // 2
            vparts = [(Vt[:, vc0 + i, :], 128) for i in range(4)] + [(Vt[0:64, vc0 + 4, :], 64)] + \
                     [(Vt[:, i, :], 128) for i in range(NCTX)]
            softmax_pv(832, 7, kparts, vparts, qc, oT[hd][:, qc * 128:(qc + 1) * 128], slot)
        if ctx_out:
            for cq in range(NCTX):
                slot = cq % 2
                qap = QT[:, cq * 128:(cq + 1) * 128]
                K.mm(pA[:, 0:NCTX * 128], qap, KTc, True, True, [bQ, bK], [bPA])
                K.copy("vector", S_sb[slot][:, 0:NCTX * 128], pA[:, 0:NCTX * 128], [bPA], [bS[slot]])
                kparts = [(i * 128, 128) for i in range(NCTX)]
                vparts = [(Vt[:, i, :], 128) for i in range(NCTX)]
                softmax_pv(NCTX * 128, NCTX, kparts, vparts, cq, oT[hd][:, cq * 128:(cq + 1) * 128], slot)

    P.add("gpsimd", lambda e: e.collective_compute("AllGather", ALU.bypass, replica_groups=[list(range(NCORE))],
                                                   ins=[dr["ag_o_in"].opt()], outs=[dr["ag_o"].opt()]),
          reads=dr["b_ag_o_in"], writes=[dr["b_ag_o"]], cc=True)

def phase_dense(env, dr, l, xsrc, bxsrc, xdst, bxdst):
    K, P, sb, pb, bP = env.K, env.P, env.sb, env.pb, env.bP
    bC = Buf("c")
    mvs, gn, ones = load_mods(env, dr, l, bC)
    wmerge, wout, w1, w2 = [dr[k][l] for k in ("wmerge", "wout", "w1", "w2")]
    c1 = sb("c1", [128, 2, 16], F32)
    c2 = sb("c2", [128, 2, 16], F32)
    c3 = sb("c3", [128, 2, 16], F32)
    sh2 = sb("sh2", [128, 2, 16], F32)
    for kind in range(2):
        K.tt("vector", c16(c1, kind), mv_all(mvs, 2, kind), gn16(gn, 1), ALU.mult, [bC], [bC])
        K.ts("vector", c16(c2, kind), mv_all(mvs, 4, kind), 1.0, None, ALU.add, None, [bC], [bC])
        K.tt("vector", c16(c2, kind), c16(c2, kind), gn16(gn, 2), ALU.mult, [bC], [bC])
        K.tt("vector", c16(c3, kind), mv_all(mvs, 5, kind), gn16(gn, 3), ALU.mult, [bC], [bC])
        K.copy("vector", c16(sh2, kind), mv_all(mvs, 3, kind), [bC], [bC])

    arena = sb("arena_d", [128, 64, TP], BF16)
    bAr = Buf("arena")
    mh = sb("mh", [128, 16, TP], BF16)
    bMh = Buf("mh")
    fb = sb("fbuf", [128, 16, TP], F32)
    bFb = Buf("fbuf")
    xs = sb("xs", [128, 16, TP], F32)
    bX = Buf("xs")
    sq = [sb("sq%d" % i, [128, TP], F32) for i in range(2)]
    bSq = [Buf("sq0"), Buf("sq1")]
    rstd = sb("rstd", [128, TP], F32)
    bR = Buf("rstd")
    tmp = [sb("tmp%d" % i, [128, TP], F32) for i in range(2)]
    bT = [Buf("t0"), Buf("t1")]
    sg = [sb("sg%d" % i, [128, TP], F32) for i in range(2)]
    bSg = [Buf("sg0"), Buf("sg1")]
    m1 = [sb("m1%d" % i, [128, TP], F32) for i in range(2)]
    bM1 = [Buf("m10"), Buf("m11")]
    m2 = [sb("m2%d" % i, [128, TP], F32) for i in range(2)]
    bM2 = [Buf("m20"), Buf("m21")]
    WS = 10240
    NWS = 4
    wsl = [sb("wsl%d" % i, [128, WS], BF16) for i in range(NWS)]
    bWs = [Buf("w%d" % i) for i in range(NWS)]
    wctr = [0]

    def wslot():
        s = wctr[0] % NWS
        wctr[0] += 1
        return s

    def load_w(s, off, wd, k0, nk, c0, ncol):
        view = wsl[s][:, off:off + nk * ncol].rearrange("p (k n) -> p k n", n=ncol)
        K.dma("gpsimd", view, wd[k0 * 128:(k0 + nk) * 128, c0:c0 + ncol].rearrange("(k p) n -> p k n", p=128),
              writes=[bWs[s]])
        return view

    def wv(s, off, nk, ncol):
        return wsl[s][:, off:off + nk * ncol].rearrange("p (k n) -> p k n", n=ncol)

    def mm_acc(pbi, wview, nk, col0, in_blocks, bin_, s):
        for k in range(nk):
            K.mm(pb[pbi][:, 0:TP], wview[:, k, col0:col0 + 128], in_blocks[k], k == 0, k == nk - 1,
                 [bWs[s], bin_], [bP[pbi]])

    agg = dr["ag_g"].rearrange("(b p) t -> b p t", p=128)
    ago = dr["ag_o"].rearrange("(b p) t -> b p t", p=128)
    agu_own = dr["ag_u_in"].rearrange("(b p) t -> b p t", p=128)

    def dyn_load(dst, src, nblk, p):
        for lo, hi, kind in tok_ranges(p):
            n = hi - lo

            def fn(e, lo=lo, n=n):
                g0 = dr["dynv"][(p, lo)]
                return e.dma_start(out=dst[:, :, lo:lo + n],
                                   in_=src[:, :, bass.ds(g0, n)].rearrange("b p t -> p b t"))
            P.add("sync", fn, reads=[dr["b_ag_g"], dr["b_ag_o"]], writes=[bAr], dma=True)

    for p in range(NPASS):
        tsl = slice(p * TP, (p + 1) * TP)
        gT = arena[:, 0:32, :]
        oT = arena[:, 32:48, :]
        uT = arena[:, 48:64, :]
        dyn_load(gT, agg, 32, p)
        dyn_load(oT, ago, 16, p)
        K.dma("sync", uT, agu_own[:, :, tsl].rearrange("b p t -> p b t"), reads=[dr["b_ag_u_in"]], writes=[bAr])
        K.dma("sync", xs[:], xsrc[:, :, tsl].rearrange("b p t -> p b t"), reads=[bxsrc], writes=[bX])
        for g in range(16):
            s = wslot()
            K.dma("gpsimd", wsl[s][:, 0:10240].rearrange("p (q m) -> p q m", m=512),
                  wmerge[g].rearrange("(q p) m -> p q m", p=128), writes=[bWs[s]])
            vs, va, vn, vb = wv(s, 0, 32, 128), wv(s, 4096, 16, 128), wv(s, 6144, 16, 128), wv(s, 8192, 16, 128)
            cb = g
            q = (cb % 2) * 4
            e = cb % 2
            mm_acc(q + 0, vs, 32, 0, [gT[:, k, :] for k in range(32)], bAr, s)
            mm_acc(q + 1, va, 16, 0, [uT[:, k, :] for k in range(16)], bAr, s)
            mm_acc(q + 2, vn, 16, 0, [oT[:, k, :] for k in range(16)], bAr, s)
            mm_acc(q + 3, vb, 16, 0, [uT[:, k, :] for k in range(16)], bAr, s)
            K.act(sg[e][:], pb[q + 1][:, 0:TP], AF.Sigmoid, [bP[q + 1]], [bSg[e]])
            K.tt("vector", m1[e][:], sg[e][:], pb[q + 0][:, 0:TP], ALU.mult, [bSg[e], bP[q + 0]], [bM1[e]])
            K.act(sg[e][:], pb[q + 3][:, 0:TP], AF.Sigmoid, [bP[q + 3]], [bSg[e]])
            K.tt("vector", m2[e][:], sg[e][:], pb[q + 2][:, 0:TP], ALU.mult, [bSg[e], bP[q + 2]], [bM2[e]])
            K.tt("gpsimd", mh[:, cb, :], m1[e][:], m2[e][:], ALU.add, [bM1[e], bM2[e]], [bMh])
        for g in range(4):
            s = wslot()
            v = load_w(s, 0, wout, 0, 16, g * 512, 512)
            for j in range(4):
                cb = g * 4 + j
                q = cb % 8
                mm_acc(q, v, 16, j * 128, [mh[:, k, :] for k in range(16)], bMh, s)
                K.copy("scalar", fb[:, cb, :], pb[q][:, 0:TP], [bP[q]], [bFb])
        rms_stats(K, [fb[:, b, :] for b in range(16)], bFb, TP, ones, bC, sq, bSq, pb[0], bP[0], rstd, bR)
        for b in range(16):
            t = b % 2
            K.tt("vector", tmp[t][:], fb[:, b, :], rstd[:], ALU.mult, [bFb, bR], [bT[t]])
            for lo, hi, kind in tok_ranges(p):
                K.stt("vector", xs[:, b, lo:hi], tmp[t][:, lo:hi], c1[:, kind, b:b + 1], xs[:, b, lo:hi],
                      ALU.mult, ALU.add, [bT[t], bC, bX], [bX])
        rms_stats(K, [xs[:, b, :] for b in range(16)], bX, TP, ones, bC, sq, bSq, pb[0], bP[0], rstd, bR)
        for b in range(16):
            t = b % 2
            K.tt("vector", tmp[t][:], xs[:, b, :], rstd[:], ALU.mult, [bX, bR], [bT[t]])
            for lo, hi, kind in tok_ranges(p):
                K.ts("gpsimd", mh[:, b, lo:hi], tmp[t][:, lo:hi], c2[:, kind, b:b + 1], sh2[:, kind, b:b + 1],
                     ALU.mult, ALU.add, [bT[t], bC], [bMh])
        aT = arena
        for g in range(16):
            s = wslot()
            v = load_w(s, 0, w1, 0, 16, g * 512, 512)
            for j in range(4):
                fbk = g * 4 + j
                q = fbk % 8
                e = fbk % 2
                mm_acc(q, v, 16, j * 128, [mh[:, k, :] for k in range(16)], bMh, s)
                K.act(sg[e][:], pb[q][:, 0:TP], AF.Relu, [bP[q]], [bSg[e]])
                K.tt("vector", aT[:, fbk, :], sg[e][:], sg[e][:], ALU.mult, [bSg[e]], [bAr])
        for g in range(16):
            s = wslot()
            K.dma("gpsimd", wsl[s][:, 0:8192].rearrange("p (q m) -> p q m", m=512),
                  w2[g].rearrange("(q p) m -> p q m", p=128), writes=[bWs[s]])
            v = wv(s, 0, 64, 128)
            cb = g
            q = cb % 8
            mm_acc(q, v, 64, 0, [aT[:, k, :] for k in range(64)], bAr, s)
            K.copy("scalar", fb[:, cb, :], pb[q][:, 0:TP], [bP[q]], [bFb])
        rms_stats(K, [fb[:, b, :] for b in range(16)], bFb, TP, ones, bC, sq, bSq, pb[0], bP[0], rstd, bR)
        for b in range(16):
            t = b % 2
            K.tt("vector", tmp[t][:], fb[:, b, :], rstd[:], ALU.mult, [bFb, bR], [bT[t]])
            for lo, hi, kind in tok_ranges(p):
                K.stt("vector", xs[:, b, lo:hi], tmp[t][:, lo:hi], c3[:, kind, b:b + 1], xs[:, b, lo:hi],
                      ALU.mult, ALU.add, [bT[t], bC, bX], [bX])
        K.dma("sync", xdst[:, :, tsl].rearrange("b p t -> p b t"), xs[:], reads=[bX], writes=[bxdst])


def build_fused():
    nc = bass.Bass("TRN2", target_bir_lowering=False, num_devices=NCORE)
    din = lambda name, shape, dt: nc.dram_tensor(name, list(shape), dt, kind="ExternalInput").ap()
    dint = lambda name, shape, dt: nc.dram_tensor(name, list(shape), dt, kind="Internal").ap()
    L = NLAYER
    dr = {}
    dr["c2"] = din("c2", [128, 16, 2], F32)
    dr["wada"] = din("wada", [L, 2048, 1536], F32)
    dr["bada"] = din("bada", [128, L, 12], F32)
    dr["gains"] = din("gains", [L, 128, 4, 16], F32)
    dr["ones"] = din("ones", [128, 128], F32)
    xT = din("xT", [16, 128, TT], F32)
    dr["w_ssd"] = din("w_ssd", [L, 2048, 1296], F32)
    dr["convw"] = din("convw", [L, 128, 6, 5], F32)
    dr["convb"] = din("convb", [L, 128, 6], F32)
    dr["hv"] = din("hv", [L, 3, 16], F32)
    dr["normw"] = din("normw", [L, 512], F32)
    dr["cos"] = din("cos", [128, NLAT * 128], F32)
    dr["sin"] = din("sin", [128, NLAT * 128], F32)
    dr["cn"] = {n: din(n, s, dt) for n, s, dt in [
        ("trif", (128, 128), F32), ("trib", (128, 128), F32), ("maskf", (128, 128), F32),
        ("maskb", (128, 128), F32), ("identf", (128, 128), F32), ("identb", (128, 128), BF16),
        ("sel", (16, 2048), F32), ("rt", (128, 128), F32)]}
    dr["cn"]["ones"] = dr["ones"]
    dr["w_na"] = din("w_na", [L, 2048, 768], F32)
    dr["nabias"] = din("nabias", [L, 2, 5, 128, 832], F32)
    for k, shp in (("wmerge", [L, 16, 2560, 512]), ("wout", [L, 2048, 2048]), ("w1", [L, 2048, 8192]),
                   ("w2", [L, 16, 2048, 512])):
        dr[k] = din(k, shp, F32)
    xo = nc.dram_tensor("xTo", [16, 128, TT], F32, kind="ExternalOutput").ap()
    dr["ag_m_in"] = dint("ag_m_in", [L * 128, 24], F32)
    dr["ag_m"] = dint("ag_m", [NCORE * L * 128, 24], F32)
    dr["ag_u_in"] = dint("ag_u_in", [16 * 128, TT], BF16)
    dr["ag_u"] = dint("ag_u", [NCORE * 16 * 128, TT], BF16)
    dr["ag_g_in"] = dint("ag_g_in", [4 * 128, TALL], BF16)
    dr["ag_g"] = dint("ag_g", [NCORE * 4 * 128, TALL], BF16)
    dr["ag_o_in"] = dint("ag_o_in", [2 * 128, TALL], BF16)
    dr["ag_o"] = dint("ag_o", [NCORE * 2 * 128, TALL], BF16)
    xres = dint("xres", [16, 128, TT], F32)
    dr["sc_xs"] = dint("sc_xs", [NCH, 128, 512], F32)
    dr["sc_B"] = dint("sc_B", [NCH, 128, 128], BF16)
    dr["sc_CT"] = dint("sc_CT", [NCH, 128, 128], BF16)
    dr["sc_cbb"] = dint("sc_cbb", [NCH, 128, 128], F32)
    dr["sc_yf"] = dint("sc_yf", [NCH, 128, 512], F32)
    dr["bSc"] = {n: [Buf("%s%d" % (n, c)) for c in range(NCH)] for n in ("xs", "B", "CT", "cbb", "yf")}
    for n in ("ag_m_in", "ag_m", "ag_u_in", "ag_u", "ag_g", "ag_o"):
        dr["b_" + n] = Buf(n)
    dr["b_ag_g_in"] = [Buf("ggin%d" % c) for c in range(NCH)]
    dr["b_ag_o_in"] = [Buf("goin%d" % c) for c in range(NCH)]
    bXin, bXres, bXo = Buf("xin"), Buf("xres"), Buf("xo")
    dr["dynv"] = {}

    def sync_prologue(eng):
        pid = eng.partition_id()
        for p in range(NPASS):
            for lo, hi, kind in tok_ranges(p):
                if kind == 0:
                    expr = pid * NLATC + (NCTX * 128 + p * TP + lo)
                else:
                    expr = pid * 32 + (p * TP + lo - NLATC)
                dr["dynv"][(p, lo)] = eng.snap(expr)

    with contextlib.ExitStack() as st:
        env = Env(nc, st)
        env.P.prologue["sync"] = sync_prologue
        phase_mod(env, dr)
        for l in range(L):
            last = l == L - 1
            xsrc, bxs = (xT, bXin) if l == 0 else (xres, bXres)
            xdst, bxd = (xo, bXo) if last else (xres, bXres)
            env.phase()
            phase_normA(env, dr, l, xsrc, bxs)
            env.phase()
            phase_ssd(env, dr, l)
            env.phase()
            phase_na(env, dr, l, not last)
            env.phase()
            phase_dense(env, dr, l, xsrc, bxs, xdst, bxd)
        env.P.emit()
    return nc


def ssd_cols(j):
    z = np.arange(512 * j, 512 * j + 512)
    x = 4096 + np.arange(512 * j, 512 * j + 512)
    B = 4096 + 4096 + np.arange(128 * j, 128 * j + 128)
    C = 4096 + 5120 + np.arange(128 * j, 128 * j + 128)
    dtf = 4096 + 6144 + np.arange(8 * j, 8 * j + 8)
    dtb = 4096 + 6144 + 64 + np.arange(8 * j, 8 * j + 8)
    return np.concatenate([z, x, B, C, dtf, dtb])


def na_cols(j):
    cols = []
    base = 4096 + 6144 + 128
    for hd in (2 * j, 2 * j + 1):
        for part in range(3):
            cols.append(base + part * 2048 + hd * 128 + np.arange(128))
    return np.concatenate(cols)


def to_fm(a):
    T, F = a.shape
    return np.ascontiguousarray(a.reshape(T, F // 128, 128).transpose(1, 2, 0))


def from_fm(a):
    nb, _, T = a.shape
    return a.transpose(2, 0, 1).reshape(T, nb * 128)


_NC = []


def kernel(**inp):
    inp = {k: np.asarray(v) for k, v in inp.items()}
    L = NLAYER
    f32 = np.float32
    x = inp["x"][0]
    ctx = inp["ctx"][0]
    consts = ssd_consts()
    cos, sin = rope_tables(NLAT * 128)
    gc0 = 4096 + 6144 + 128 + 3 * 2048
    shared = {
        "c2": np.ascontiguousarray(np.stack([inp["c"][0], inp["c_ctx"]], -1).reshape(16, 128, 2).transpose(1, 0, 2)).astype(f32),
        "gains": np.ascontiguousarray(np.stack([np.stack([inp[k][l].reshape(16, 128) for k in
                 ("g_pre_mix", "g_post_mix", "g_pre_mlp", "g_post_mlp")]).transpose(2, 0, 1) for l in range(L)])).astype(f32),
        "ones": np.ones((128, 128), f32), "cos": cos, "sin": sin,
        "wout": inp["w_out"], "w1": inp["w_mlp1"],
    }

    def tile_q(wm):
        Lw, Kd, Nd = wm.shape
        t = wm.reshape(Lw, Kd // 512, 4, 128, Nd // 128, 128).transpose(0, 4, 1, 3, 2, 5)
        return np.ascontiguousarray(t).reshape(Lw, Nd // 128, (Kd // 512) * 128, 512)

    wgm = inp["w_in"][:, :, gc0:gc0 + 4096]
    shared["wmerge"] = np.ascontiguousarray(np.concatenate(
        [tile_q(inp["w_ssd_o"]), tile_q(wgm[:, :, 0:2048]), tile_q(inp["w_na_o"]), tile_q(wgm[:, :, 2048:4096])], axis=2))
    shared["w2"] = tile_q(inp["w_mlp2"])
    for n in ("trif", "trib", "maskf", "maskb", "identf", "identb", "sel", "rt"):
        shared[n] = consts[n]
    maps = []
    for j in range(NCORE):
        d = dict(shared)
        mc = (np.arange(6)[:, None, None] * 2048 + (2 * j + np.arange(2))[None, :, None] * 128 + np.arange(128)[None, None, :]).reshape(-1)
        d["wada"] = np.ascontiguousarray(inp["w_ada"][:, :, mc])
        d["bada"] = np.ascontiguousarray(inp["b_ada"][:, mc].reshape(L, 12, 128).transpose(2, 0, 1)).astype(f32)
        d["xT"] = to_fm(np.concatenate([x[1024 * j:1024 * (j + 1)], ctx[32 * j:32 * (j + 1)]], 0)).astype(f32)
        sc = ssd_cols(j)
        d["w_ssd"] = np.ascontiguousarray(inp["w_in"][:, :, sc])
        cch = sc[512:512 + 768] - 4096
        d["convw"] = np.ascontiguousarray(inp["conv_w"][:, :, cch].transpose(0, 2, 1).reshape(L, 6, 128, 5).transpose(0, 2, 1, 3))
        d["convb"] = np.ascontiguousarray(inp["conv_b"][:, cch].reshape(L, 6, 128).transpose(0, 2, 1))
        d["hv"] = np.ascontiguousarray(np.stack([inp[k][:, :, 8 * j:8 * j + 8].reshape(L, 16) for k in
                                                 ("dt_bias", "a_log", "d_skip")], 1)).astype(f32)
        d["normw"] = np.ascontiguousarray(inp["ssd_norm"][:, 512 * j:512 * j + 512])
        d["w_na"] = np.ascontiguousarray(inp["w_in"][:, :, na_cols(j)])
        d["nabias"] = np.stack([np.stack([na_bias(inp["rpb"][l][2 * j + i], NLAT) for i in range(2)]) for l in range(L)])
        maps.append(d)
    if not _NC:
        _NC.append(build_fused())
    res = run_bass_kernel_spmd(_NC[0], maps, core_ids=list(range(NCORE)))
    xn = [from_fm(res.results[j]["xTo"]) for j in range(NCORE)]
    out = np.concatenate([t[:1024] for t in xn], 0)
    return np.ascontiguousarray(out[None].astype(np.float32))
```
